# Optimizing a Trainium2 kernel written in Bass

```python
import jax, jax.numpy as jnp
from jax import lax
import numpy as np

D_MODEL = 2048
BATCH = 4
SEQ = 2048
DEPTH = 1

FOX_HEAD_DIM = 128
FOX_WIDTH = D_MODEL // 2
FOX_HEADS = FOX_WIDTH // FOX_HEAD_DIM
RWKV_HEAD_DIM = 64
RWKV_WIDTH = D_MODEL // 2
RWKV_HEADS = RWKV_WIDTH // RWKV_HEAD_DIM
DECAY_LORA = max(32, int(round(D_MODEL ** 0.5 * 1.8 / 32)) * 32)
AAA_LORA = max(32, int(round(D_MODEL ** 0.5 * 1.8 / 32)) * 32)
Q_BLOCK = 128
RMS_EPS = 1e-6
GN_EPS = 64e-5
L2_EPS = 1e-12

FOX_SIZES = (FOX_WIDTH, FOX_WIDTH, FOX_WIDTH, FOX_WIDTH, FOX_HEADS)
RWKV_SIZES = (RWKV_WIDTH, RWKV_WIDTH, RWKV_WIDTH, RWKV_WIDTH, DECAY_LORA, AAA_LORA)
FOX_COLS = sum(FOX_SIZES)
RWKV_COLS = sum(RWKV_SIZES)
IN_COLS = FOX_COLS + RWKV_COLS + 2 * D_MODEL

kernel_name = "fox_rwkv7_gated_parallel_hybrid"


def _split(u, sizes):
    idx = [int(i) for i in np.cumsum(sizes)[:-1]]
    return jnp.split(u, idx, axis=-1)


def _rmsnorm(x, g):
    xf = x.astype(jnp.float32)
    y = xf * lax.rsqrt(jnp.mean(xf * xf, axis=-1, keepdims=True) + RMS_EPS)
    return (y * g.astype(jnp.float32)).astype(x.dtype)


def _token_shift(u, mu):
    prev = jnp.pad(u, ((0, 0), (1, 0), (0, 0)))[:, :-1]
    return u + (prev - u) * mu


def _fox_attention(q, k, v, log_f):
    c = jnp.cumsum(log_f, axis=-1)
    T = q.shape[2]
    scale = FOX_HEAD_DIM ** -0.5
    outs = []
    for i in range(T // Q_BLOCK):
        s0, e = i * Q_BLOCK, (i + 1) * Q_BLOCK
        qb, kb, vb = q[:, :, s0:e], k[:, :, :e], v[:, :, :e]
        logits = (jnp.einsum('bhqd,bhkd->bhqk', qb, kb) * scale
                  + c[:, :, s0:e, None] - c[:, :, None, :e])
        causal = jnp.arange(e)[None, :] <= jnp.arange(s0, e)[:, None]
        logits = jnp.where(causal, logits, -jnp.inf)
        p = jax.nn.softmax(logits, axis=-1)
        outs.append(jnp.einsum('bhqk,bhkd->bhqd', p, vb))
    return jnp.concatenate(outs, axis=2)


def _rwkv7_scan(r, decay, k, v, kk, b):
    B, T, H, N = r.shape
    S0 = jnp.zeros((B, H, N, N), jnp.float32)
    xs = tuple(jnp.moveaxis(t, 1, 0) for t in (r, decay, k, v, kk, b))

    def step(S, inp):
        r_t, w_t, k_t, v_t, kk_t, b_t = inp
        sa = jnp.einsum('bhvk,bhk->bhv', S, -kk_t)
        S = (S * w_t[:, :, None, :] + sa[..., None] * b_t[:, :, None, :]
             + v_t[..., None] * k_t[:, :, None, :])
        y = jnp.einsum('bhvk,bhk->bhv', S, r_t)
        return S, y

    _, ys = lax.scan(step, S0, xs)
    return jnp.moveaxis(ys, 0, 1)


def _hybrid_layer(x, norm_gain, w_in, fox_forget_bias, rwkv_shift_mix, rwkv_w0, rwkv_w2,
                  rwkv_a0, rwkv_a2, rwkv_k_k, rwkv_k_a, rwkv_r_k, rwkv_ln_w, rwkv_ln_b,
                  w_proj_fox, w_proj_rwkv, w_out):
    B, T, _ = x.shape
    f32 = jnp.float32
    h = _rmsnorm(x, norm_gain)
    u = h @ w_in
    u_fox, u_rwkv, g_a, g_b = _split(u, (FOX_COLS, RWKV_COLS, D_MODEL, D_MODEL))

    q, k, v, z_a, f_logit = _split(u_fox.astype(f32), FOX_SIZES)
    to_heads = lambda t: t.reshape(B, T, FOX_HEADS, FOX_HEAD_DIM).transpose(0, 2, 1, 3)
    log_f = jax.nn.log_sigmoid(f_logit + fox_forget_bias.astype(f32)).transpose(0, 2, 1)
    o_a = _fox_attention(to_heads(q), to_heads(k), to_heads(v), log_f)
    o_a = o_a.transpose(0, 2, 1, 3).reshape(B, T, FOX_WIDTH) * jax.nn.silu(z_a)

    u_rwkv = _token_shift(u_rwkv.astype(f32), rwkv_shift_mix.astype(f32))
    r, kr, vr, z_b, w_down, a_down = _split(u_rwkv, RWKV_SIZES)
    w = -jax.nn.softplus(-(rwkv_w0.astype(f32) + jnp.tanh(w_down) @ rwkv_w2.astype(f32))) - 0.5
    decay = jnp.exp(-jnp.exp(w))
    a = jax.nn.sigmoid(rwkv_a0.astype(f32) + a_down @ rwkv_a2.astype(f32))
    heads = lambda t: t.reshape(B, T, RWKV_HEADS, RWKV_HEAD_DIM)
    kk = heads(kr * rwkv_k_k.astype(f32))
    kk = kk / jnp.maximum(jnp.sqrt(jnp.sum(kk * kk, axis=-1, keepdims=True)), L2_EPS)
    kr = kr * (1.0 + (a - 1.0) * rwkv_k_a.astype(f32))
    rh, kh, vh, ah = heads(r), heads(kr), heads(vr), heads(a)
    y = _rwkv7_scan(rh, heads(decay), kh, vh, kk, kk * ah)
    mu = jnp.mean(y, axis=-1, keepdims=True)
    var = jnp.mean(jnp.square(y - mu), axis=-1, keepdims=True)
    y = ((y - mu) * lax.rsqrt(var + GN_EPS)).reshape(B, T, RWKV_WIDTH)
    y = y * rwkv_ln_w.astype(f32) + rwkv_ln_b.astype(f32)
    bonus = jnp.sum(rh * kh * rwkv_r_k.astype(f32), axis=-1, keepdims=True) * vh
    o_b = (y + bonus.reshape(B, T, RWKV_WIDTH)) * jax.nn.silu(z_b)

    o_a = o_a.astype(x.dtype) @ w_proj_fox
    o_b = o_b.astype(x.dtype) @ w_proj_rwkv
    m = jax.nn.sigmoid(g_a) * o_a + jax.nn.sigmoid(g_b) * o_b
    return x + m @ w_out


def setup_inputs(seed: int = 0) -> dict:
    key = jax.random.key(seed)
    ks = jax.random.split(key, 20)
    n = jax.random.normal
    L, D = DEPTH, D_MODEL
    return {
        "x": n(ks[0], (BATCH, SEQ, D), jnp.float32),
        "norm_gain": 1.0 + 0.05 * n(ks[1], (L, D), jnp.float32),
        "w_in": n(ks[2], (L, D, IN_COLS), jnp.float32) * D ** -0.5,
        "fox_forget_bias": 3.0 + 0.5 * n(ks[3], (L, FOX_HEADS), jnp.float32),
        "rwkv_shift_mix": jax.random.uniform(ks[4], (L, RWKV_COLS), jnp.float32),
        "rwkv_w0": jax.random.uniform(ks[5], (L, RWKV_WIDTH), jnp.float32, -6.5, -1.5),
        "rwkv_w2": n(ks[6], (L, DECAY_LORA, RWKV_WIDTH), jnp.float32) * 0.5 * DECAY_LORA ** -0.5,
        "rwkv_a0": 0.1 * n(ks[7], (L, RWKV_WIDTH), jnp.float32),
        "rwkv_a2": n(ks[8], (L, AAA_LORA, RWKV_WIDTH), jnp.float32) * AAA_LORA ** -0.5,
        "rwkv_k_k": 0.85 + 0.05 * n(ks[9], (L, RWKV_WIDTH), jnp.float32),
        "rwkv_k_a": 1.0 + 0.05 * n(ks[10], (L, RWKV_WIDTH), jnp.float32),
        "rwkv_r_k": -0.04 + 0.02 * n(ks[11], (L, RWKV_HEADS, RWKV_HEAD_DIM), jnp.float32),
        "rwkv_ln_w": 1.0 + 0.05 * n(ks[12], (L, RWKV_WIDTH), jnp.float32),
        "rwkv_ln_b": 0.02 * n(ks[13], (L, RWKV_WIDTH), jnp.float32),
        "w_proj_fox": n(ks[14], (L, FOX_WIDTH, D), jnp.float32) * FOX_WIDTH ** -0.5,
        "w_proj_rwkv": n(ks[15], (L, RWKV_WIDTH, D), jnp.float32) * RWKV_WIDTH ** -0.5,
        "w_out": n(ks[16], (L, D, D), jnp.float32) * D ** -0.5,
        "final_norm_gain": 1.0 + 0.05 * n(ks[17], (D,), jnp.float32),
    }


def reference(x, norm_gain, w_in, fox_forget_bias, rwkv_shift_mix, rwkv_w0, rwkv_w2,
              rwkv_a0, rwkv_a2, rwkv_k_k, rwkv_k_a, rwkv_r_k, rwkv_ln_w, rwkv_ln_b,
              w_proj_fox, w_proj_rwkv, w_out, final_norm_gain):
    h = x
    for l in range(DEPTH):
        h = _hybrid_layer(h, norm_gain[l], w_in[l], fox_forget_bias[l], rwkv_shift_mix[l],
                          rwkv_w0[l], rwkv_w2[l], rwkv_a0[l], rwkv_a2[l], rwkv_k_k[l],
                          rwkv_k_a[l], rwkv_r_k[l], rwkv_ln_w[l], rwkv_ln_b[l],
                          w_proj_fox[l], w_proj_rwkv[l], w_out[l])
    return _rmsnorm(h, final_norm_gain)
```

```python
import os
import numpy as np
import concourse.bass as bass
import concourse.mybir as mybir
from concourse.bass_utils import run_bass_kernel_spmd
from contextlib import ExitStack

F32 = mybir.dt.float32
BF16 = mybir.dt.bfloat16
AF = mybir.ActivationFunctionType
ALU = mybir.AluOpType
AX = mybir.AxisListType

D = 2048
T = 2048
NCORE = 8
R0 = 4104
G0 = 4104 + 4288
NFC = 60
CH = 64
NCH = T // CH
TH = T // 2
NCHH = TH // CH
RMS_EPS = 1e-6
GN_EPS = 64e-5
DEBUG = os.environ.get("KDEBUG", "")
KSTOP = int(os.environ.get("KSTOP", "99"))
KITER = int(os.environ.get("KITER", "1"))


class _Stop(Exception):
    pass


class Buf:
    __slots__ = ("w", "r")

    def __init__(self):
        self.w = None
        self.r = {}


class Eng:
    def __init__(self, name, h, sem, same):
        self.name = name
        self.h = h
        self.sem = sem
        self.cnt = 0
        self.waited = {}
        self.same = same
        self.ring = []
        self.dma_i = 0


class K:
    def __init__(self, nc, es):
        self.nc = nc
        mk = lambda n: es.enter_context(nc.semaphore(n))
        self.pe = Eng("pe", nc.tensor, mk("s_pe"), False)
        self.act = Eng("act", nc.scalar, mk("s_act"), True)
        self.dve = Eng("dve", nc.vector, mk("s_dve"), True)
        self.pool = Eng("pool", nc.gpsimd, mk("s_pool"), True)
        self.sp = Eng("sp", nc.sync, mk("s_sp"), False)
        self.engs = [self.pe, self.act, self.dve, self.pool, self.sp]
        self.log = {e.name: [] for e in self.engs}
        self.sp.ring = [[mk("d_sp%d" % i), 0] for i in range(20)]
        self.pool.ring = [[mk("d_pl%d" % i), 0] for i in range(12)]

    def wait(self, E, dep):
        key, sem, val = dep
        if E.waited.get(key, 0) >= val:
            return
        E.h.wait_ge(sem, val)
        self.log[E.name].append(('w', id(sem), val))
        E.waited[key] = val

    def _deps(self, E, rd, wr, name=None):
        name = E.name if name is None else name
        for b in rd:
            if b.w is not None:
                if b.w[0] == name:
                    if E.same:
                        self.wait(E, b.w)
                else:
                    self.wait(E, b.w)
        for b in wr:
            if b.w is not None and b.w[0] != name:
                self.wait(E, b.w)
            for d in b.r.values():
                if d[0] != name:
                    self.wait(E, d)

    def _mark(self, dep, rd, wr):
        for b in rd:
            b.r[dep[0]] = dep
        for b in wr:
            b.w = dep
            b.r = {}

    def op(self, E, fn, rd=(), wr=()):
        self._deps(E, rd, wr)
        inst = fn()
        inst.then_inc(E.sem, 1)
        self.log[E.name].append(('i', id(E.sem), 1))
        E.cnt += 1
        self._mark((E.name, E.sem, E.cnt), rd, wr)

    def dma(self, Q, out, in_, rd=(), wr=(), **kw):
        slot = Q.dma_i % len(Q.ring)
        Q.dma_i += 1
        sem, issued = Q.ring[slot]
        key = ("dma", Q.name, slot)
        if issued > 0:
            self.wait(Q, (key, sem, issued * 16))
        self._deps(Q, rd, wr, name='__dma__')
        inst = Q.h.dma_start(out=out, in_=in_, **kw)
        inst.then_inc(sem, 16)
        self.log[Q.name].append(('i', id(sem), 16))
        Q.ring[slot][1] += 1
        dep = (key, sem, Q.ring[slot][1] * 16)
        self._mark(dep, rd, wr)
        return dep

    def barrier(self, dma_queues=("sp",)):
        for E in self.engs:
            for X in self.engs:
                if X is not E and X.cnt > 0:
                    self.wait(E, (X.name, X.sem, X.cnt))
            for Q in self.engs:
                if Q.name in dma_queues:
                    for slot, (sem, issued) in enumerate(Q.ring):
                        if issued > 0:
                            self.wait(E, (("dma", Q.name, slot), sem, issued * 16))


def build():
    nc = bass.Bass("TRN2", target_bir_lowering=False)
    es = ExitStack()
    k = K(nc, es)
    _CACHE['k'] = k
    try:
        _build_inner(nc, es, k)
        es.close()
    except _Stop:
        k.barrier(dma_queues=("sp", "pool"))
    return nc


def _build_inner(nc, es, k):
    PE, ACT, DVE, POOL, SP = k.pe, k.act, k.dve, k.pool, k.sp

    def din(name, shape, dt=F32):
        return nc.dram_tensor(name, list(shape), dt, kind="ExternalInput").ap()

    xT = din("xT", [D, T])
    xTg = din("xTg", [D, T // 2])
    xtok = din("xtok", [T // 2, D])
    wfm = din("wfm", [NFC, 128, 2048])
    wlora = din("wlora", [2, 128, 16 * 96])
    wv_d = din("wv", [128, 16 * 516])
    w2_d = din("w2", [96, 512])
    a2_d = din("a2", [96, 512])
    wpf_d = din("wpf", [16, 128, 1024])
    wpr_d = din("wpr", [16, 128, 1024])
    wout_d = din("wout", [4, 128, 16 * 512])
    pp_d = din("pp", [128, 64])
    lnwb_d = din("lnwb", [4, 128, 64])
    lnbb_d = din("lnbb", [4, 128, 64])
    fgb_d = din("fgb", [128, D])
    fbias_d = din("fbias", [128, 4])
    cst_d = din("cst", [128, 2304])
    rmask_d = din("rmask", [128, TH])
    sel_d = din("sel", [128, 2])
    out_d = nc.dram_tensor("out", [T // 2, D], F32, kind="ExternalOutput").ap()
    cinA = nc.dram_tensor("cinA", [512, T], BF16)
    coutA = nc.dram_tensor("coutA", [1024, T], BF16)
    cinB = nc.dram_tensor("cinB", [512, T], BF16)
    coutB = nc.dram_tensor("coutB", [1024, T], BF16)
    cin_v = [cinA.ap().rearrange("(c p) t -> c p t", c=4, p=128), cinB.ap().rearrange("(c p) t -> c p t", c=4, p=128)]
    RG = [[0, 1], [2, 3], [4, 5], [6, 7]]
    ccsA = es.enter_context(nc.semaphore("ccsA"))
    ccsB = es.enter_context(nc.semaphore("ccsB"))

    nctr = {"i": 0}

    def sb(name, shape, dt, st=es):
        nctr["i"] += 1
        return st.enter_context(nc.sbuf_tensor("s%d_%s" % (nctr["i"], name), list(shape), dt))

    ps = [es.enter_context(nc.psum_tensor("ps%d" % i, [128, 512], F32)) for i in range(8)]
    pb = [Buf() for _ in range(8)]
    ps7b = ps[7].bitcast(BF16)
    ps6b = ps[6].bitcast(BF16)
    dbg_d = nc.dram_tensor("dbg", [128, T], F32, kind="ExternalOutput").ap() if DEBUG else None

    def tap(name, ap, P=128, n=T, view=None):
        if DEBUG != name:
            return
        k.barrier(dma_queues=("sp", "pool"))
        o = dbg_d[0:P, 0:n]
        if view is not None:
            o = o.rearrange(view[0], **view[1])
        d = k.dma(POOL, o, ap)
        k.wait(POOL, d)
        raise _Stop()

    wbuf = [sb("wbuf%d" % i, [128, 2048], BF16) for i in range(4)]
    wbb = [Buf() for _ in range(4)]
    pp = sb("pp", [128, 64], F32)
    omka = sb("omka", [128, 4], F32)
    omu = sb("omu", [128, 18], F32)
    sel = sb("sel", [128, 2], F32)
    cident = sb("cident", [128, 128], BF16)
    ctri = sb("ctri", [128, 128], BF16)
    cones = sb("cones", [128, 128], BF16)
    cbones = sb("cbones", [128, 128], BF16)
    cistack = sb("cistack", [128, 64], BF16)
    cmnl = sb("cmnl", [128, 512], BF16)
    cma3 = sb("cma3", [128, 384], BF16)
    ctriu32 = sb("ctriu32", [128, 128], F32)
    cones32 = sb("cones32", [128, 128], F32)
    cb = Buf()

    k.dma(SP, pp[:], pp_d[:, :], wr=[cb])
    k.dma(SP, sel[:], sel_d[:, :], wr=[cb])
    k.dma(SP, ctriu32[:], cst_d[:, 1920:2048], wr=[cb])
    k.dma(SP, cones32[:], cst_d[:, 2048:2176], wr=[cb])
    k.dma(POOL, cident[:], cst_d[:, 0:128], wr=[cb])
    k.dma(POOL, ctri[:], cst_d[:, 128:256], wr=[cb])
    k.dma(POOL, cones[:], cst_d[:, 256:384], wr=[cb])
    k.dma(POOL, cbones[:], cst_d[:, 384:512], wr=[cb])
    k.dma(POOL, cistack[:], cst_d[:, 512:576], wr=[cb])
    k.dma(POOL, cmnl[:], cst_d[:, 576:1088], wr=[cb])
    k.dma(POOL, cma3[:], cst_d[:, 1088:1472], wr=[cb])
    PG, PMU, PMUL, PW0, PA0, PKK, PKA, PRK = 0, 16, 32, 34, 38, 42, 48, 52
    k.op(DVE, lambda: nc.vector.tensor_scalar(omka[:], pp[:, PKA:PKA + 4], -1.0, 1.0, ALU.mult, ALU.add),
         rd=[cb], wr=[cb])
    k.op(DVE, lambda: nc.vector.tensor_scalar(omu[:], pp[:, PMU:PMU + 18], -1.0, 1.0, ALU.mult, ALU.add),
         rd=[cb], wr=[cb])

    wseq = list(range(28)) + [x for cc in range(16) for x in (28 + cc, 44 + cc)]
    wstate = {"next": 0}

    def wprefetch(upto):
        while wstate["next"] <= min(upto, len(wseq) - 1):
            i = wstate["next"]
            k.dma(POOL, wbuf[i % 4][:], wfm[wseq[i], :, :], wr=[wbb[i % 4]])
            wstate["next"] += 1

    pset = {"i": 0}

    def proj_fm(fc, hsrc, ntb=4, tb0=0, pf=2):
        wprefetch(fc + pf)
        base = 4 * (pset["i"] % 2)
        pset["i"] += 1
        w = wbuf[fc % 4]
        for dc in range(16):
            for tb in range(ntb):
                bk = base + tb
                k.op(PE, lambda bk=bk, dc=dc, tb=tb: nc.tensor.matmul(
                    ps[bk][:, :], w[:, dc * 128:(dc + 1) * 128],
                    hsrc[:, dc, (tb0 + tb) * 512:(tb0 + tb + 1) * 512],
                    start=(dc == 0), stop=(dc == 15)),
                    rd=[wbb[fc % 4]], wr=[pb[bk]])
        return [base + tb for tb in range(ntb)]

    def make_hT(dst, src_d, ntb, st):
        xs = [sb("xs%d" % i, [128, 16, 512], F32, st) for i in range(2)]
        xsb = [[Buf() for _ in range(16)] for _ in range(2)]
        sq = [sb("sq%d" % i, [128, 512], BF16, st) for i in range(3)]
        sqb = [Buf() for _ in range(3)]
        rs_ = sb("rs_", [128, 512], F32, st)
        rsb = Buf()
        for tb in range(ntb):
            X = xs[tb % 2]
            Xb = xsb[tb % 2]
            for dc in range(16):
                k.dma(SP, X[:, dc, :], src_d[dc * 128:(dc + 1) * 128, tb * 512:(tb + 1) * 512], wr=[Xb[dc]])
            bk = tb % 2
            for dc in range(16):
                s = (tb * 16 + dc) % 3
                k.op(ACT, lambda dc=dc, s=s: nc.scalar.activation(sq[s][:], X[:, dc, :], AF.Square),
                     rd=[Xb[dc]], wr=[sqb[s]])
                k.op(PE, lambda dc=dc, s=s: nc.tensor.matmul(ps[bk][:, :], cones[:], sq[s][:],
                                                             start=(dc == 0), stop=(dc == 15)),
                     rd=[sqb[s], cb], wr=[pb[bk]])
            k.op(ACT, lambda: nc.scalar.activation(rs_[:], ps[bk][:, :], AF.Sqrt, bias=RMS_EPS, scale=1.0 / D),
                 rd=[pb[bk]], wr=[rsb])
            k.op(DVE, lambda: nc.vector.reciprocal(rs_[:], rs_[:]), rd=[rsb], wr=[rsb])
            for dc in range(16):
                k.op(DVE, lambda dc=dc: nc.vector.scalar_tensor_tensor(
                    dst[:, dc, tb * 512:(tb + 1) * 512], X[:, dc, :], pp[:, PG + dc:PG + dc + 1], rs_[:],
                    ALU.mult, ALU.mult), rd=[Xb[dc], rsb, cb], wr=[])

    wprefetch(1)
    with ExitStack() as sm:
        hT = sb("hT", [128, 16, T], BF16, sm)
        rawr = [sb("rawr%d" % i, [128, 513], F32, sm) for i in range(2)]
        rawb = [Buf() for _ in range(2)]
        carry = sb("carry", [128, 8], F32, sm)
        carb = Buf()
        twd = sb("twd", [96, T], BF16, sm)
        adb = sb("adb", [96, T], BF16, sm)
        w2b = sb("w2b", [96, 512], BF16, sm)
        a2b = sb("a2b", [96, 512], BF16, sm)
        lb = Buf()
        k.dma(POOL, w2b[:], w2_d[:, :], wr=[lb])
        k.dma(POOL, a2b[:], a2_d[:, :], wr=[lb])
        sfox = ExitStack()
        wv = sb("wvs", [128, 16 * 516], BF16, sfox)
        wvb = Buf()
        k.dma(POOL, wv[:], wv_d[:, :], wr=[wvb], max_dma_last_dim=8192)
        with ExitStack() as st:
            make_hT(hT, xT, 4, st)
            k.barrier()
            tap('hT0', hT[:, 0, :])
            tap('hT15', hT[:, 15, :])
            if KSTOP <= 1:
                raise _Stop()

        rstate = {"i": 0}

        def shift_blocks(P, banks, dst, dstb, mucol, ccol, first):
            for bi, bk in enumerate(banks):
                R = rawr[rstate["i"] % 2]
                Rb = rawb[rstate["i"] % 2]
                rstate["i"] += 1
                if bi == 0:
                    if first:
                        k.op(DVE, lambda R=R: nc.vector.memset(R[0:P, 0:1], 0.0), wr=[Rb])
                    else:
                        k.op(DVE, lambda R=R: nc.vector.tensor_copy(R[0:P, 0:1], carry[0:P, ccol:ccol + 1]),
                             rd=[carb], wr=[Rb])
                else:
                    k.op(DVE, lambda R=R, Rp=Rp: nc.vector.tensor_copy(R[0:P, 0:1], Rp[0:P, 512:513]),
                         rd=[Rpb], wr=[Rb])
                k.op(ACT, lambda R=R, bk=bk: nc.scalar.copy(R[0:P, 1:513], ps[bk][0:P, :]), rd=[pb[bk]], wr=[Rb])
                sl = slice(bi * 512, (bi + 1) * 512)
                k.op(DVE, lambda R=R, sl=sl: nc.vector.tensor_scalar(
                    dst[0:P, sl], R[0:P, 1:513], omu[0:P, mucol:mucol + 1], None, ALU.mult), rd=[Rb, cb], wr=[dstb])
                k.op(DVE, lambda R=R, sl=sl: nc.vector.scalar_tensor_tensor(
                    dst[0:P, sl], R[0:P, 0:512], pp[0:P, PMU + mucol:PMU + mucol + 1], dst[0:P, sl],
                    ALU.mult, ALU.add), rd=[Rb, dstb, cb], wr=[dstb])
                Rp, Rpb = R, Rb
            k.op(DVE, lambda: nc.vector.tensor_copy(carry[0:P, ccol:ccol + 1], Rp[0:P, 512:513]),
                 rd=[Rpb], wr=[carb])

        with ExitStack() as st:
            wl = sb("wl", [128, 16 * 96], BF16, st)
            wlb = Buf()
            shf = sb("lshf", [96, T], F32, st)
            shfb = Buf()
            for li in range(2):
                k.dma(POOL, wl[:], wlora[li, :, :], wr=[wlb])
                for dc in range(16):
                    for tb in range(4):
                        k.op(PE, lambda dc=dc, tb=tb, li=li: nc.tensor.matmul(
                            ps[4 * li + tb][0:96, :], wl[:, dc * 96:(dc + 1) * 96], hT[:, dc, tb * 512:(tb + 1) * 512],
                            start=(dc == 0), stop=(dc == 15)), rd=[wlb], wr=[pb[4 * li + tb]])
                shift_blocks(96, [4 * li + b_ for b_ in range(4)], shf, shfb, 16 + li, 4 + li, True)
                if li == 0:
                    k.op(ACT, lambda: nc.scalar.activation(twd[:], shf[:], AF.Tanh), rd=[shfb], wr=[lb])
                else:
                    k.op(ACT, lambda: nc.scalar.copy(adb[:], shf[:]), rd=[shfb], wr=[lb])
            k.barrier()
            tap('twd', twd[:], 96)
            tap('adb', adb[:], 96)
            if KSTOP <= 2:
                raise _Stop()

        with ExitStack() as st:
            V = sb("V", [128, 16, 4, 132], BF16, st)
            Vb = Buf()
            fbias = sb("fbias", [128, 4], F32, st)
            k.dma(SP, fbias[:], fbias_d[:, :], wr=[Vb])
            xall = sb("xall", [128, 16, 4], F32, st)
            spall = sb("spall", [128, 16, 4], F32, st)
            xab = Buf()
            k.op(POOL, lambda: nc.gpsimd.memset(V[:], 1.0), wr=[Vb])
            for tt in range(16):
                bv = tt % 4
                bf = 4 + tt % 2
                for dc in range(16):
                    k.op(PE, lambda dc=dc: nc.tensor.matmul(
                        ps[bv][:, :], hT[:, dc, tt * 128:(tt + 1) * 128], wv[:, dc * 516:dc * 516 + 512],
                        start=(dc == 0), stop=(dc == 15)), rd=[wvb], wr=[pb[bv]])
                for dc in range(16):
                    k.op(PE, lambda dc=dc: nc.tensor.matmul(
                        ps[bf][:, 0:4], hT[:, dc, tt * 128:(tt + 1) * 128], wv[:, dc * 516 + 512:dc * 516 + 516],
                        start=(dc == 0), stop=(dc == 15)), rd=[wvb], wr=[pb[bf]])
                k.op(ACT, lambda: nc.scalar.copy(V[:, tt, :, 0:128], ps[bv][:, :].rearrange("p (h d) -> p h d", h=4)),
                     rd=[pb[bv]], wr=[Vb])
                k.op(DVE, lambda: nc.vector.tensor_tensor(xall[:, tt, :], ps[bf][:, 0:4], fbias[:], ALU.add),
                     rd=[pb[bf], Vb], wr=[xab])
            k.op(ACT, lambda: nc.scalar.activation(spall[:], xall[:], AF.Exp, scale=-1.0), rd=[xab], wr=[xab])
            k.op(ACT, lambda: nc.scalar.activation(spall[:], spall[:], AF.Ln, bias=1.0), rd=[xab], wr=[xab])
            spf = spall[:].rearrange("p a b -> p (a b)")
            sph = [sb("sph%d" % i, [128, 64], BF16, st) for i in range(3)]
            spr = sb("spr", [128, 64], F32, st)
            spt = sb("spt", [128, 64], F32, st)
            k.op(DVE, lambda: nc.vector.tensor_copy(sph[0][:], spf), rd=[xab], wr=[xab])
            k.op(DVE, lambda: nc.vector.tensor_copy(spt[:], sph[0][:]), rd=[xab], wr=[xab])
            k.op(DVE, lambda: nc.vector.tensor_tensor(spr[:], spf, spt[:], ALU.subtract), rd=[xab], wr=[xab])
            k.op(DVE, lambda: nc.vector.tensor_copy(sph[1][:], spr[:]), rd=[xab], wr=[xab])
            k.op(DVE, lambda: nc.vector.tensor_copy(spt[:], sph[1][:]), rd=[xab], wr=[xab])
            k.op(DVE, lambda: nc.vector.tensor_tensor(spr[:], spr[:], spt[:], ALU.subtract), rd=[xab], wr=[xab])
            k.op(DVE, lambda: nc.vector.tensor_copy(sph[2][:], spr[:]), rd=[xab], wr=[xab])
            for i in range(3):
                k.op(PE, lambda i=i: nc.tensor.matmul(ps[5][:, 0:64], ctri[:], sph[i][:], start=(i == 0), stop=(i == 2)),
                     rd=[xab, cb], wr=[pb[5]])
            for i in range(3):
                k.op(PE, lambda i=i: nc.tensor.matmul(ps[6][:, 0:64], cones[:], sph[i][:], start=(i == 0), stop=(i == 2)),
                     rd=[xab, cb], wr=[pb[6]])
            pre = sb("pre", [128, 17, 4], F32, st)
            cinc = sb("cinc", [128, 16, 4], F32, st)
            tot = sb("tot", [128, 16, 4], F32, st)
            cbf = Buf()
            k.op(DVE, lambda: nc.vector.tensor_copy(tot[:].rearrange("p a b -> p (a b)"), ps[6][:, 0:64]),
                 rd=[pb[6]], wr=[cbf])
            k.op(DVE, lambda: nc.vector.memset(pre[:, 0, :], 0.0), wr=[cbf])
            for tt in range(16):
                k.op(DVE, lambda tt=tt: nc.vector.tensor_tensor(pre[:, tt + 1, :], pre[:, tt, :], tot[:, tt, :], ALU.add),
                     rd=[cbf], wr=[cbf])
            k.op(DVE, lambda: nc.vector.tensor_tensor(cinc[:].rearrange("p a b -> p (a b)"), ps[5][:, 0:64],
                                                      pre[:, 0:16, :].rearrange("p a b -> p (a b)"), ALU.add),
                 rd=[pb[5], cbf], wr=[cbf])
            fb = sb("fb", [128, 4, 8, 16], F32, st)
            for h in range(4):
                for hb in range(8):
                    k.op(DVE, lambda h=h, hb=hb: nc.vector.tensor_scalar(
                        fb[:, h, hb, :], cinc[:, :, h], pre[:, 2 * hb + 1, h:h + 1], None, ALU.subtract),
                        rd=[cbf], wr=[cbf])
            k.barrier()
            tap('V0', V[:, :, 0, 0:128], view=('p (a b) -> p a b', dict(a=16)))
            tap('cinc', cinc[:].rearrange('p a b -> p (a b)'), 128, 64)
            tap('fb', fb[:].rearrange('p a b c -> p (a b c)'), 128, 512)
            if KSTOP <= 3:
                raise _Stop()

            qT = sb("qT", [128, T], BF16, st)
            kT = sb("kT", [128, T], BF16, st)
            szaT = sb("szaT", [128, T], BF16, st)
            oaS = sb("oaS", [128, T], BF16, st)
            qb_, kb_, zb_, oab = Buf(), Buf(), Buf(), Buf()
            pT = [sb("pT%d" % i, [128, 512], BF16, st) for i in range(3)]
            pTb = [Buf() for _ in range(3)]
            onrm = [sb("onrm%d" % i, [128, 128], BF16, st) for i in range(4)]
            onb = [Buf() for _ in range(4)]
            rinv = [sb("rinv%d" % i, [128, 1], F32, st) for i in range(4)]
            pending = []

            def flush_T(upto):
                while pending and pending[0][0] <= upto:
                    _, tq_, s_, jj = pending.pop(0)
                    k.op(PE, lambda s_=s_: nc.tensor.transpose(ps7b[:, 0:128], onrm[s_][:], cident[:]),
                         rd=[onb[s_]], wr=[pb[7]])
                    k.op(DVE, lambda tq_=tq_: nc.vector.tensor_tensor(
                        oaS[:, tq_ * 128:(tq_ + 1) * 128], ps7b[:, 0:128], szaT[:, tq_ * 128:(tq_ + 1) * 128],
                        ALU.mult), rd=[pb[7], zb_], wr=[oab])
            scale = 128 ** -0.5
            for j in range(4):
                for kind, dst, dstb in ((0, qT, qb_), (1, kT, kb_), (2, szaT, zb_)):
                    banks = proj_fm(3 * j + kind, hT)
                    for tb, bk in enumerate(banks):
                        if kind == 2:
                            k.op(ACT, lambda tb=tb, bk=bk, dst=dst: nc.scalar.activation(
                                dst[:, tb * 512:(tb + 1) * 512], ps[bk][:, :], AF.Silu), rd=[pb[bk]], wr=[dstb])
                        elif tb % 2 == 0:
                            k.op(ACT, lambda tb=tb, bk=bk, dst=dst: nc.scalar.copy(
                                dst[:, tb * 512:(tb + 1) * 512], ps[bk][:, :]), rd=[pb[bk]], wr=[dstb])
                        else:
                            k.op(DVE, lambda tb=tb, bk=bk, dst=dst: nc.vector.tensor_copy(
                                dst[:, tb * 512:(tb + 1) * 512], ps[bk][:, :]), rd=[pb[bk]], wr=[dstb])
                steps = [(qb, kt) for qb in range(4) for kt in range(4 * qb + 4)]

                def issue_S(i):
                    qb, kt = steps[i]
                    jd = max(0, kt - 4 * qb)
                    q0 = qb * 512 + jd * 128
                    n = 512 - jd * 128
                    bk = i % 3
                    k.op(PE, lambda: nc.tensor.matmul(ps[bk][:, 0:n], kT[:, kt * 128:(kt + 1) * 128], qT[:, q0:q0 + n],
                                                      start=True, stop=True), rd=[kb_, qb_], wr=[pb[bk]])

                def issue_rest(i):
                    qb, kt = steps[i]
                    jd = max(0, kt - 4 * qb)
                    n = 512 - jd * 128
                    bk = i % 3
                    P_ = pT[i % 3]
                    Pb = pTb[i % 3]
                    for pr in range(2):
                        qis = [qi for qi in (2 * pr, 2 * pr + 1) if qi >= jd]
                        if not qis:
                            continue
                        cs = slice((qis[0] - jd) * 128, (qis[-1] - jd + 1) * 128)
                        k.op(ACT, lambda pr=pr, cs=cs: nc.scalar.activation(
                            P_[:, cs], ps[bk][:, cs], AF.Exp, bias=fb[:, j, 2 * qb + pr, kt:kt + 1], scale=scale),
                            rd=[pb[bk]], wr=[Pb])
                    if kt >= 4 * qb:
                        k.op(DVE, lambda: nc.vector.tensor_tensor(P_[:, 0:128], P_[:, 0:128], ctri[:], ALU.mult),
                             rd=[Pb], wr=[Pb])
                    for qi in range(jd, 4):
                        ob = 3 + qi
                        last = (kt == 4 * qb + qi)
                        k.op(PE, lambda qi=qi, ob=ob, last=last: nc.tensor.matmul(
                            ps[ob][:, 0:129], P_[:, (qi - jd) * 128:(qi - jd + 1) * 128], V[:, kt, j, 0:129],
                            start=(kt == 0), stop=last), rd=[Pb], wr=[pb[ob]])
                        if last:
                            tq = 4 * qb + qi
                            s = tq % 4
                            k.op(DVE, lambda ob=ob, s=s: nc.vector.reciprocal(rinv[s][:], ps[ob][:, 128:129]),
                                 rd=[pb[ob]], wr=[onb[s]])
                            k.op(DVE, lambda ob=ob, s=s: nc.vector.tensor_scalar(
                                onrm[s][:], ps[ob][:, 0:128], rinv[s][:, 0:1], None, ALU.mult),
                                rd=[pb[ob], onb[s]], wr=[onb[s]])
                            pending.append((i, tq, s, j))

                n_st = len(steps)
                issue_S(0)
                issue_S(1)
                for i in range(n_st):
                    if i + 2 < n_st:
                        issue_S(i + 2)
                    flush_T(i - 2)
                    issue_rest(i)
                flush_T(n_st)
                k.dma(SP, cin_v[0][j], oaS[:], rd=[oab])
                tap('qT', qT[:])
                tap('kT', kT[:])
                tap('szaT', szaT[:])
                tap('oaS%d' % j, oaS[:])
                if KSTOP <= 4 and j + 1 >= KITER:
                    raise _Stop()

        k.barrier()
        sfox.close()
        nc.gpsimd.collective_compute("AllGather", ALU.bypass, replica_groups=RG,
                                     ins=[cinA.ap().opt()], outs=[coutA.ap().opt()]).then_inc(ccsA, 1)
        NEG_EH = -float(np.exp(-0.5))
        with ExitStack() as st:
            f32t = lambda n: sb(n, [128, TH], F32, st)
            bf16t = lambda n: sb(n, [128, TH], BF16, st)
            rmask = bf16t("rmask")
            rmb = Buf()
            k.dma(POOL, rmask[:], rmask_d[:, :], wr=[rmb])
            rs, ks, vs = f32t("rs"), f32t("ks"), f32t("vs")
            a_, g_, t1, t2 = f32t("a_"), f32t("g_"), f32t("t1"), f32t("t2")
            kk = vs
            szbT = bf16t("szbT")
            atT, rtT, ktT, btT = bf16t("atT"), bf16t("rtT"), bf16t("ktT"), bf16t("btT")
            khT, bhT, vbT, rkT = bf16t("khT"), bf16t("bhT"), bf16t("vbT"), bf16t("rkT")
            obS = bf16t("obS")
            gam = sb("gam", [128, NCHH], F32, st)
            KBV = sb("KBV", [128, NCHH, 3, 64], BF16, st)
            A3 = sb("A3", [128, NCHH, 3, 64], BF16, st)
            TT = sb("TT", [128, NCHH, 64], BF16, st)
            NLs = [[sb("NLs%d_%d" % (b4, i), [128, 4, 2, 64], BF16, st) for i in range(2)] for b4 in range(4)]
            NLsb = [[Buf(), Buf()] for _ in range(4)]
            Ts = [[sb("Ts%d_%d" % (b4, i), [128, 4, 64], BF16, st) for i in range(2)] for b4 in range(4)]
            Tsb = [[Buf(), Buf()] for _ in range(4)]
            prodb = [Buf() for _ in range(4)]
            S32 = sb("S32", [128, 64], F32, st)
            Sbf = [sb("Sbf%d" % i, [128, 64], BF16, st) for i in range(2)]
            Xsb = sb("Xsb", [128, 64], BF16, st)
            Usb = sb("Usb", [128, 64], BF16, st)
            lnw = sb("lnw", [128, 64], F32, st)
            lnb = sb("lnb", [128, 64], F32, st)
            yc = [sb("yc%d" % i, [128, 4, 64], F32, st) for i in range(2)]
            ysq = sb("ysq", [128, 4, 64], F32, st)
            ysqb = Buf()
            Gb = [sb("Gb%d" % i, [128, 4, 64], BF16, st) for i in range(2)]
            st1 = [sb("st1_%d" % i, [128, 16], F32, st) for i in range(2)]
            B = lambda: Buf()
            rsb, ksb, vsb, ab, gb, t1b, t2b, zbb = (B() for _ in range(8))
            kkb = vsb
            atb, rtb, ktb, btb, khb, bhb, vbb, rkb, gamb, obb = (B() for _ in range(10))
            KBVb, A3b, TTb = B(), B(), B()
            S32b, Sbfb, Xb, Ub, lnb_ = B(), [B(), B()], B(), B(), B()
            ycb, Gbb, st1b = [B(), B()], [B(), B()], [B(), B()]
            sc = 0

            def stage1(j, hh):
                fc0 = 12 + 4 * j
                for kind, dst, dstb in ((0, rs, rsb), (1, ks, ksb), (2, vs, vsb), (3, t2, t2b)):
                    fc = fc0 + kind
                    if hh == 0:
                        wprefetch(fc + 3 - kind)
                    w = wbuf[fc % 4]
                    n = 0
                    for dc in range(16):
                        for tb in range(2):
                            bk = 4 + tb
                            k.op(PE, lambda bk=bk, dc=dc, tb=tb, w=w: nc.tensor.matmul(
                                ps[bk][:, :], w[:, dc * 128:(dc + 1) * 128],
                                hT[:, dc, (2 * hh + tb) * 512:(2 * hh + tb + 1) * 512],
                                start=(dc == 0), stop=(dc == 15)), rd=[wbb[fc % 4]], wr=[pb[bk]])
                            n += 1
                            if n % 4 == 0:
                                yield
                    shift_blocks(128, [4, 5], dst, dstb, kind * 4 + j, kind, hh == 0)
                    yield
                if hh == 1:
                    wprefetch(fc0 + 3 + 3)

            its = [(j, hh) for j in range(4) for hh in range(2)]
            for _ in stage1(*its[0]):
                pass
            for it_i, (j, hh) in enumerate(its):
                if True:
                    if hh == 0:
                        k.dma(SP, lnw[:], lnwb_d[j, :, :], wr=[lnb_])
                        k.dma(SP, lnb[:], lnbb_d[j, :, :], wr=[lnb_])
                    t0 = hh * TH
                    nxt = stage1(*its[it_i + 1]) if it_i + 1 < len(its) else iter(())
                    k.op(ACT, lambda: nc.scalar.activation(szbT[:], t2[:], AF.Silu), rd=[t2b], wr=[zbb])
                    k.op(ACT, lambda: nc.scalar.copy(vbT[:], vs[:]), rd=[vsb], wr=[vbb])
                    for tb in range(2):
                        tsl = slice(t0 + tb * 512, t0 + (tb + 1) * 512)
                        k.op(PE, lambda tb=tb, tsl=tsl: nc.tensor.matmul(ps[tb][:, :], w2b[:, j * 128:(j + 1) * 128],
                                                                         twd[:, tsl], start=True, stop=True),
                             rd=[lb], wr=[pb[tb]])
                        k.op(PE, lambda tb=tb, tsl=tsl: nc.tensor.matmul(ps[4 + tb][:, :], a2b[:, j * 128:(j + 1) * 128],
                                                                         adb[:, tsl], start=True, stop=True),
                             rd=[lb], wr=[pb[4 + tb]])
                    for tb in range(2):
                        k.op(ACT, lambda tb=tb: nc.scalar.activation(t2[:, tb * 512:(tb + 1) * 512], ps[tb][:, :], AF.Sigmoid,
                                                                     bias=pp[:, PW0 + j:PW0 + j + 1]), rd=[pb[tb], zbb], wr=[t2b])
                        k.op(ACT, lambda tb=tb: nc.scalar.activation(a_[:, tb * 512:(tb + 1) * 512], ps[4 + tb][:, :], AF.Sigmoid,
                                                                     bias=pp[:, PA0 + j:PA0 + j + 1]), rd=[pb[4 + tb]], wr=[ab])
                    k.op(DVE, lambda: nc.vector.tensor_scalar(t2[:], t2[:], NEG_EH, None, ALU.mult), rd=[t2b], wr=[t2b])
                    k.op(DVE, lambda: nc.vector.tensor_tensor_scan(g_[:], rmask[:], t2[:], 0.0, ALU.mult, ALU.add),
                         rd=[rmb, t2b], wr=[gb])
                    k.op(DVE, lambda: nc.vector.tensor_scalar(kk[:], ks[:], pp[:, PKK + j:PKK + j + 1], None, ALU.mult),
                         rd=[ksb, vbb], wr=[kkb])
                    k.op(ACT, lambda: nc.scalar.activation(khT[:], kk[:], AF.Square), rd=[kkb], wr=[khb])
                    for tb in range(2):
                        k.op(PE, lambda tb=tb: nc.tensor.matmul(ps[tb][:, :], cbones[:], khT[:, tb * 512:(tb + 1) * 512],
                                                                start=True, stop=True), rd=[khb, cb], wr=[pb[tb]])
                        k.op(ACT, lambda tb=tb: nc.scalar.activation(t1[:, tb * 512:(tb + 1) * 512], ps[tb][:, :], AF.Sqrt),
                             rd=[pb[tb]], wr=[t1b])
                    k.op(DVE, lambda: nc.vector.tensor_scalar(t1[:], t1[:], 1e-12, None, ALU.max), rd=[t1b], wr=[t1b])
                    k.op(DVE, lambda: nc.vector.reciprocal(t1[:], t1[:]), rd=[t1b], wr=[t1b])
                    k.op(DVE, lambda: nc.vector.tensor_tensor(kk[:], kk[:], t1[:], ALU.mult), rd=[kkb, t1b], wr=[kkb])
                    k.op(DVE, lambda: nc.vector.tensor_scalar(t1[:], a_[:], pp[:, PKA + j:PKA + j + 1], omka[:, j:j + 1],
                                                              ALU.mult, ALU.add), rd=[ab, t1b, kkb], wr=[t1b])
                    k.op(DVE, lambda: nc.vector.tensor_tensor(ks[:], ks[:], t1[:], ALU.mult), rd=[ksb, t1b], wr=[ksb])
                    k.op(POOL, lambda: nc.gpsimd.tensor_tensor(a_[:], a_[:], kk[:], ALU.mult), rd=[ab, kkb, t1b], wr=[ab])
                    k.op(DVE, lambda: nc.vector.scalar_tensor_tensor(rkT[:], rs[:], pp[:, PRK + j:PRK + j + 1], ks[:],
                                                                     ALU.mult, ALU.mult), rd=[rsb, ksb], wr=[rkb])
                    k.op(ACT, lambda: nc.scalar.activation(t1[:], g_[:], AF.Exp), rd=[gb, ksb], wr=[t1b])
                    k.op(DVE, lambda: nc.vector.tensor_tensor(rtT[:], rs[:], t1[:], ALU.mult), rd=[rsb, t1b], wr=[rtb])
                    k.op(DVE, lambda: nc.vector.tensor_copy(gam[:], t1[:].rearrange("p (c i) -> p c i", i=CH)[:, :, CH - 1]),
                         rd=[t1b], wr=[gamb])
                    k.op(POOL, lambda: nc.gpsimd.tensor_tensor(t2[:], g_[:], t2[:], ALU.subtract), rd=[gb, t2b], wr=[t2b])
                    k.op(ACT, lambda: nc.scalar.activation(t1[:], t2[:], AF.Exp), rd=[t2b, rtb, gamb], wr=[t1b])
                    k.op(DVE, lambda: nc.vector.scalar_tensor_tensor(atT[:], kk[:], -1.0, t1[:], ALU.mult, ALU.mult),
                         rd=[kkb, t1b], wr=[atb])
                    k.op(ACT, lambda: nc.scalar.activation(t1[:], g_[:], AF.Exp, scale=-1.0), rd=[gb, atb], wr=[t1b])
                    k.op(DVE, lambda: nc.vector.tensor_tensor(ktT[:], ks[:], t1[:], ALU.mult), rd=[ksb, t1b], wr=[ktb])
                    k.op(POOL, lambda: nc.gpsimd.tensor_tensor(btT[:], a_[:], t1[:], ALU.mult), rd=[ab, t1b], wr=[btb])
                    g3 = g_[:].rearrange("p (c i) -> p c i", i=CH)
                    k.op(DVE, lambda: nc.vector.tensor_tensor(
                        t2[:].rearrange("p (c i) -> p c i", i=CH), g3[:, :, CH - 1:CH].broadcast_to([128, NCHH, CH]), g3,
                        ALU.subtract), rd=[gb, t2b], wr=[t2b])
                    k.op(ACT, lambda: nc.scalar.activation(t2[:], t2[:], AF.Exp), rd=[t2b], wr=[t2b])
                    k.op(DVE, lambda: nc.vector.tensor_tensor(khT[:], ks[:], t2[:], ALU.mult), rd=[ksb, t2b], wr=[khb])
                    k.op(POOL, lambda: nc.gpsimd.tensor_tensor(bhT[:], a_[:], t2[:], ALU.mult), rd=[ab, t2b], wr=[bhb])

                    for c0 in range(0, NCHH, 4):
                        tbank = 7 - (c0 // 4) % 2
                        tview = ps7b if tbank == 7 else ps6b
                        for ci in range(4):
                            c = c0 + ci
                            for qi, (src, srcb) in enumerate(((khT, khb), (bhT, bhb), (vbT, vbb))):
                                for h in range(2):
                                    col = (ci * 3 + qi) * 64
                                    k.op(PE, lambda h=h, c=c, col=col, src=src, tview=tview: nc.tensor.transpose(
                                        tview[64 * h:64 * h + 64, col:col + 64], src[64 * h:64 * h + 64, c * CH:(c + 1) * CH],
                                        cident[64 * h:64 * h + 64, 64 * h:64 * h + 64]), rd=[srcb, cb], wr=[pb[tbank]])
                        k.op(DVE, lambda c0=c0, tview=tview: nc.vector.tensor_copy(
                            KBV[:, c0:c0 + 4, :, :].rearrange("p a b c -> p (a b c)"), tview[:, 0:768]),
                            rd=[pb[tbank]], wr=[KBVb])

                    for b4 in range(4):
                        c0 = 4 * b4
                        for ci in range(4):
                            c = c0 + ci
                            tk = slice(c * CH, (c + 1) * CH)
                            for h in range(2):
                                hp = slice(64 * h, 64 * h + 64)
                                k.op(PE, lambda hp=hp, tk=tk, ci=ci: nc.tensor.matmul(
                                    ps[b4][hp, (ci * 2) * 64:(ci * 2 + 1) * 64], btT[hp, tk], atT[hp, tk],
                                    start=True, stop=True), rd=[btb, atb], wr=[pb[b4]])
                                k.op(PE, lambda hp=hp, tk=tk, ci=ci: nc.tensor.matmul(
                                    ps[b4][hp, (ci * 2 + 1) * 64:(ci * 2 + 2) * 64], atT[hp, tk], btT[hp, tk],
                                    start=True, stop=True), rd=[btb, atb], wr=[pb[b4]])
                        k.op(DVE, lambda b4=b4: nc.vector.tensor_tensor(
                            NLs[b4][0][:].rearrange("p a b c -> p (a b c)"), ps[b4][:, :], cmnl[:], ALU.mult),
                            rd=[pb[b4], cb], wr=[NLsb[b4][0]])
                        for half in range(2):
                            ba = 4 + half
                            for cj in range(2):
                                c = c0 + half * 2 + cj
                                tk = slice(c * CH, (c + 1) * CH)
                                for h in range(2):
                                    hp = slice(64 * h, 64 * h + 64)
                                    for qi, (lt, ltb, rt, rtb_) in enumerate(((ktT, ktb, atT, atb), (ktT, ktb, rtT, rtb),
                                                                              (btT, btb, rtT, rtb))):
                                        col = (cj * 3 + qi) * 64
                                        k.op(PE, lambda hp=hp, tk=tk, col=col, lt=lt, rt=rt, ba=ba: nc.tensor.matmul(
                                            ps[ba][hp, col:col + 64], lt[hp, tk], rt[hp, tk], start=True, stop=True),
                                            rd=[ltb, rtb_], wr=[pb[ba]])
                            cc0 = c0 + half * 2
                            k.op(DVE, lambda cc0=cc0, ba=ba: nc.vector.tensor_tensor(
                                A3[:, cc0:cc0 + 2, :, :].rearrange("p a b c -> p (a b c)"), ps[ba][:, 0:384], cma3[:], ALU.mult),
                                rd=[pb[ba], cb], wr=[A3b])
                        k.op(POOL, lambda b4=b4: nc.gpsimd.tensor_tensor(
                            Ts[b4][0][:], NLs[b4][0][:, :, 0, :],
                            cistack[:].rearrange("p (o c) -> p o c", o=1).broadcast_to([128, 4, 64]),
                            ALU.add), rd=[NLsb[b4][0], cb], wr=[Tsb[b4][0]])
                    cu = 0
                    for lev in range(1, 6):
                        if lev > int(os.environ.get('KLEV', '9')):
                            break
                        nx = 1 - cu
                        for b4 in range(4):
                            cur = NLs[b4][cu]
                            for ci in range(4):
                                for h in range(2):
                                    hp = slice(64 * h, 64 * h + 64)
                                    if lev < 5:
                                        k.op(PE, lambda hp=hp, ci=ci, cur=cur, b4=b4: nc.tensor.matmul(
                                            ps[b4][hp, (ci * 2) * 64:(ci * 2 + 1) * 64], cur[hp, ci, 1, :], cur[hp, ci, 0, :],
                                            start=True, stop=True), rd=[NLsb[b4][cu]], wr=[pb[b4]])
                                    k.op(PE, lambda hp=hp, ci=ci, cur=cur, b4=b4: nc.tensor.matmul(
                                        ps[b4][hp, (ci * 2 + 1) * 64:(ci * 2 + 2) * 64], cur[hp, ci, 0, :], cur[hp, ci, 1, :],
                                        start=True, stop=True), rd=[NLsb[b4][cu]], wr=[pb[b4]])
                            oth = NLs[b4][nx]
                            if lev < 5:
                                k.op(ACT, lambda oth=oth, b4=b4: nc.scalar.copy(oth[:].rearrange("p a b c -> p (a b c)"),
                                                                                ps[b4][:, :]), rd=[pb[b4]], wr=[NLsb[b4][nx]])
                            else:
                                k.op(ACT, lambda oth=oth, b4=b4: nc.scalar.copy(
                                    oth[:, :, 1, :], ps[b4][:, :].rearrange("p (a b c) -> p a b c", a=4, b=2)[:, :, 1, :]),
                                    rd=[pb[b4]], wr=[NLsb[b4][nx]])
                        for b4 in range(4):
                            new_ = NLs[b4][nx]
                            tc_ = Ts[b4][cu]
                            pbank = 4 + b4
                            pc0 = 0
                            for ci in range(4):
                                for h in range(2):
                                    hp = slice(64 * h, 64 * h + 64)
                                    k.op(PE, lambda hp=hp, ci=ci, new_=new_, tc_=tc_, pbank=pbank, pc0=pc0: nc.tensor.matmul(
                                        ps[pbank][hp, pc0 + ci * 64:pc0 + (ci + 1) * 64], new_[hp, ci, 1, :], tc_[hp, ci, :],
                                        start=True, stop=True), rd=[NLsb[b4][nx], Tsb[b4][cu]], wr=[pb[pbank], prodb[b4]])
                            if lev < 5:
                                k.op(DVE, lambda b4=b4, tc_=tc_, pbank=pbank, pc0=pc0: nc.vector.tensor_tensor(
                                    Ts[b4][nx][:].rearrange("p a c -> p (a c)"), ps[pbank][:, pc0:pc0 + 256],
                                    tc_[:].rearrange("p a c -> p (a c)"), ALU.add),
                                    rd=[prodb[b4], pb[pbank], Tsb[b4][cu]], wr=[Tsb[b4][nx]])
                            else:
                                k.op(DVE, lambda b4=b4, tc_=tc_, pbank=pbank, pc0=pc0: nc.vector.tensor_tensor(
                                    TT[:, 4 * b4:4 * b4 + 4, :].rearrange("p a c -> p (a c)"), ps[pbank][:, pc0:pc0 + 256],
                                    tc_[:].rearrange("p a c -> p (a c)"), ALU.add),
                                    rd=[prodb[b4], pb[pbank], Tsb[b4][cu]], wr=[TTb])
                        cu = nx
                    if KSTOP <= 5:
                        raise _Stop()

                    if hh == 0:
                        k.op(DVE, lambda: nc.vector.memset(S32[:], 0.0), wr=[S32b])
                        k.op(DVE, lambda: nc.vector.memset(Sbf[sc % 2][:], 0.0), wr=[Sbfb[sc % 2]])

                    def post_stage(stg, g):
                        c0 = 4 * g
                        p = g % 2
                        yb = 2 + p
                        Y3 = ps[yb][:, 0:256].rearrange("p (a b) -> p a b", a=4)
                        bc = lambda ap: ap.rearrange("p (a o) -> p a o", o=1).broadcast_to([128, 4, 64])
                        if stg == 1:
                            k.op(DVE, lambda: nc.vector.tensor_reduce(st1[p][:, 0:4], Y3, AX.X, ALU.add),
                                 rd=[pb[yb]], wr=[st1b[p]])
                            k.op(DVE, lambda: nc.vector.tensor_scalar(st1[p][:, 0:4], st1[p][:, 0:4], -1.0 / 64, None, ALU.mult),
                                 rd=[st1b[p]], wr=[st1b[p]])
                            k.op(DVE, lambda: nc.vector.tensor_tensor(yc[p][:], Y3, bc(st1[p][:, 0:4]), ALU.add),
                                 rd=[pb[yb], st1b[p]], wr=[ycb[p]])
                            k.op(DVE, lambda: nc.vector.tensor_copy(st1[p][:, 12:16], ps[yb][:, 256:260]),
                                 rd=[pb[yb]], wr=[st1b[p]])
                            k.op(POOL, lambda: nc.gpsimd.tensor_tensor(ysq[:], yc[p][:], yc[p][:], ALU.mult),
                                 rd=[ycb[p]], wr=[ysqb])
                        elif stg == 2:
                            k.op(DVE, lambda: nc.vector.tensor_reduce(st1[p][:, 4:8], ysq[:], AX.X, ALU.add),
                                 rd=[ysqb], wr=[st1b[p]])
                            k.op(ACT, lambda: nc.scalar.activation(st1[p][:, 8:12], st1[p][:, 4:8], AF.Sqrt, bias=GN_EPS,
                                                                   scale=1.0 / 64), rd=[st1b[p]], wr=[st1b[p]])
                        elif stg == 3:
                            k.op(DVE, lambda: nc.vector.reciprocal(st1[p][:, 8:12], st1[p][:, 8:12]), rd=[st1b[p]], wr=[st1b[p]])
                            k.op(DVE, lambda: nc.vector.tensor_tensor(yc[p][:], yc[p][:], bc(st1[p][:, 8:12]), ALU.mult),
                                 rd=[ycb[p], st1b[p]], wr=[ycb[p]])
                            k.op(DVE, lambda: nc.vector.tensor_tensor(
                                yc[p][:], yc[p][:], lnw[:].rearrange("p (o c) -> p o c", o=1).broadcast_to([128, 4, 64]), ALU.mult),
                                rd=[ycb[p], lnb_], wr=[ycb[p]])
                            k.op(POOL, lambda: nc.gpsimd.tensor_tensor(
                                yc[p][:], yc[p][:], lnb[:].rearrange("p (o c) -> p o c", o=1).broadcast_to([128, 4, 64]), ALU.add),
                                rd=[ycb[p], lnb_], wr=[ycb[p]])
                            k.op(POOL, lambda: nc.gpsimd.tensor_tensor(ysq[:], KBV[:, c0:c0 + 4, 2, :], bc(st1[p][:, 12:16]), ALU.mult),
                                 rd=[KBVb, st1b[p], ysqb], wr=[ysqb])
                        elif stg == 4:
                            k.op(DVE, lambda: nc.vector.tensor_tensor(Gb[p][:], ysq[:], yc[p][:], ALU.add),
                                 rd=[ysqb, ycb[p]], wr=[Gbb[p]])
                        elif stg == 5:
                            for ci in range(4):
                                for h in range(2):
                                    hp = slice(64 * h, 64 * h + 64)
                                    k.op(PE, lambda hp=hp, ci=ci: nc.tensor.transpose(
                                        ps7b[hp, ci * 64:(ci + 1) * 64], Gb[p][hp, ci, :], cident[hp, hp]),
                                        rd=[Gbb[p], cb], wr=[pb[7]])
                        else:
                            tks = slice(c0 * CH, (c0 + 4) * CH)
                            k.op(DVE, lambda: nc.vector.tensor_tensor(obS[:, tks], ps7b[:, 0:256], szbT[:, tks], ALU.mult),
                                 rd=[pb[7], zbb], wr=[obb])

                    for c in range(NCHH):
                        g, ci = c // 4, c % 4
                        tk = slice(c * CH, (c + 1) * CH)
                        Sc, Scb = Sbf[sc % 2], Sbfb[sc % 2]
                        Sn, Snb = Sbf[(sc + 1) % 2], Sbfb[(sc + 1) % 2]
                        yb = 2 + g % 2
                        ycol = slice(ci * 64, (ci + 1) * 64)
                        sc += 1
                        for h in range(2):
                            hp = slice(64 * h, 64 * h + 64)
                            k.op(PE, lambda hp=hp: nc.tensor.matmul(ps[0][hp, 0:64], A3[hp, c, 0, :], KBV[hp, c, 2, :],
                                                                    start=True, stop=False), rd=[A3b, KBVb], wr=[pb[0]])
                            k.op(PE, lambda hp=hp: nc.tensor.matmul(ps[0][hp, 0:64], atT[hp, tk], Sc[hp, :],
                                                                    start=False, stop=True), rd=[atb, Scb], wr=[pb[0]])
                        k.op(ACT, lambda: nc.scalar.copy(Xsb[:], ps[0][:, 0:64]), rd=[pb[0]], wr=[Xb])
                        for h in range(2):
                            hp = slice(64 * h, 64 * h + 64)
                            k.op(PE, lambda hp=hp: nc.tensor.matmul(ps[yb][hp, ycol], A3[hp, c, 1, :], KBV[hp, c, 2, :],
                                                                    start=True, stop=False), rd=[A3b, KBVb], wr=[pb[yb]])
                            k.op(PE, lambda hp=hp: nc.tensor.matmul(ps[yb][hp, ycol], rtT[hp, tk], Sc[hp, :],
                                                                    start=False, stop=False), rd=[rtb, Scb], wr=[pb[yb]])
                            k.op(PE, lambda hp=hp: nc.tensor.matmul(ps[1][hp, 64:128], KBV[hp, c, 0, :], KBV[hp, c, 2, :],
                                                                    start=True, stop=False), rd=[KBVb], wr=[pb[1]])
                        next(nxt, None)
                        for h in range(2):
                            hp = slice(64 * h, 64 * h + 64)
                            k.op(PE, lambda hp=hp: nc.tensor.matmul(ps[6][hp, 0:64], TT[hp, c, :], Xsb[hp, :],
                                                                    start=True, stop=True), rd=[TTb, Xb], wr=[pb[6]])
                        k.op(DVE, lambda: nc.vector.tensor_copy(Usb[:], ps[6][:, 0:64]), rd=[pb[6]], wr=[Ub])
                        next(nxt, None)
                        for h in range(2):
                            hp = slice(64 * h, 64 * h + 64)
                            k.op(PE, lambda hp=hp: nc.tensor.matmul(ps[1][hp, 64:128], KBV[hp, c, 1, :], Usb[hp, :],
                                                                    start=False, stop=True), rd=[KBVb, Ub], wr=[pb[1]])
                        for h in range(2):
                            hp = slice(64 * h, 64 * h + 64)
                            k.op(PE, lambda hp=hp: nc.tensor.matmul(ps[yb][hp, ycol], A3[hp, c, 2, :], Usb[hp, :],
                                                                    start=False, stop=True), rd=[A3b, Ub], wr=[pb[yb]])
                            k.op(PE, lambda hp=hp: nc.tensor.matmul(ps[yb][hp, 256 + ci:257 + ci], rkT[hp, tk], cones[hp, 0:1],
                                                                    start=True, stop=True), rd=[rkb, cb], wr=[pb[yb]])
                        k.op(DVE, lambda: nc.vector.scalar_tensor_tensor(Sn[:], S32[:], gam[:, c:c + 1], ps[1][:, 64:128],
                                                                         ALU.mult, ALU.add), rd=[S32b, gamb, pb[1]], wr=[Snb])
                        k.op(DVE, lambda: nc.vector.scalar_tensor_tensor(S32[:], S32[:], gam[:, c:c + 1], ps[1][:, 64:128],
                                                                         ALU.mult, ALU.add), rd=[S32b, gamb, pb[1]], wr=[S32b])
                        next(nxt, None)
                        if g >= 2 and ci == 0:
                            post_stage(5, g - 2)
                        if g >= 2 and ci == 1:
                            post_stage(6, g - 2)
                        if g >= 1:
                            post_stage(ci + 1, g - 1)
                    for _ in nxt:
                        pass
                    post_stage(5, 2)
                    post_stage(6, 2)
                    for stg in range(1, 7):
                        post_stage(stg, 3)
                    k.dma(SP, cin_v[1][j][:, t0:t0 + TH], obS[:], rd=[obb])
                    tap('obS%d' % (2 * j + hh), obS[:], 128, TH)
                    if KSTOP <= 6 and 2 * j + hh + 1 >= KITER:
                        raise _Stop()

    k.barrier()
    nc.gpsimd.collective_compute("AllGather", ALU.bypass, replica_groups=RG,
                                 ins=[cinB.ap().opt()], outs=[coutB.ap().opt()]).then_inc(ccsB, 1)
    if KSTOP <= 7:
        raise _Stop()

    TG = T // 2
    with ExitStack() as st:
        fg = sb("fg", [128, D], F32, st)
        fgbuf = Buf()
        k.dma(SP, fg[:], fgb_d[:, :], wr=[fgbuf])
        mT = sb("mT", [128, 16, TG], BF16, st)
        mTb = [Buf() for _ in range(16)]
        with ExitStack() as st1:
            hTg = sb("hTg", [128, 16, TG], BF16, st1)
            with ExitStack() as st2:
                make_hT(hTg, xTg, 2, st2)
                nc.gpsimd.wait_ge(ccsA, 1)
                nc.gpsimd.wait_ge(ccsB, 1)
                k.op(POOL, lambda: nc.gpsimd.memset(omka[:, 0:1], 0.0), wr=[Buf()])
                k.barrier()
                tap('hTg', hTg[:, 0, :], 128, TG)
            ofull = sb("ofull", [128, 16, TG], BF16, st1)
            ofb = Buf()
            oA = [sb("oA%d" % i, [128, TG], BF16, st1) for i in range(2)]
            oB = [sb("oB%d" % i, [128, TG], BF16, st1) for i in range(2)]
            oAb = [Buf() for _ in range(2)]
            wp = [sb("wp%d" % i, [128, 2, 1024], BF16, st1) for i in range(2)]
            wpb = [Buf() for _ in range(2)]
            sg = [sb("sg%d" % i, [128, 2, 2, 512], F32, st1) for i in range(2)]
            sgb = [Buf() for _ in range(2)]
            srcs = [coutA.ap().rearrange("(q p) t -> p q t", p=128), coutB.ap().rearrange("(q p) t -> p q t", p=128)]
            for q in range(16):
                s = q % 2
                src = srcs[q // 8]
                k.dma(SP, oA[s][:], src[:, q % 8, 0:TG], wr=[oAb[s]])
                k.dma(SP, oB[s][:], src[:, q % 8, TG:T], wr=[oAb[s]])
                k.op(DVE, lambda q=q, s=s: nc.vector.tensor_scalar(ofull[:, q, :], oA[s][:], sel[:, 0:1], None, ALU.mult),
                     rd=[oAb[s], cb], wr=[ofb])
                k.op(DVE, lambda q=q, s=s: nc.vector.scalar_tensor_tensor(ofull[:, q, :], oB[s][:], sel[:, 1:2], ofull[:, q, :],
                                                                          ALU.mult, ALU.add), rd=[oAb[s], ofb, cb], wr=[ofb])
            tap('ofull0', ofull[:, 0, :], 128, TG)
            tap('ofull9', ofull[:, 9, :], 128, TG)
            for cc in range(16):
                s = cc % 2
                k.dma(POOL, wp[s][:, 0, :], wpf_d[cc, :, :], wr=[wpb[s]])
                k.dma(POOL, wp[s][:, 1, :], wpr_d[cc, :, :], wr=[wpb[s]])
                for gi in range(2):
                    fc = 28 + cc * 2 + gi
                    wprefetch(fc + 2)
                    w = wbuf[fc % 4]
                    for dc in range(16):
                        for tb in range(2):
                            k.op(PE, lambda dc=dc, gi=gi, w=w, tb=tb: nc.tensor.matmul(
                                ps[gi * 2 + tb][:, :], w[:, dc * 128:(dc + 1) * 128], hTg[:, dc, tb * 512:(tb + 1) * 512],
                                start=(dc == 0), stop=(dc == 15)), rd=[wbb[fc % 4]], wr=[pb[gi * 2 + tb]])
                for pi in range(2):
                    for kc in range(8):
                        for tb in range(2):
                            k.op(PE, lambda pi=pi, kc=kc, tb=tb: nc.tensor.matmul(
                                ps[4 + pi * 2 + tb][:, :], wp[s][:, pi, kc * 128:(kc + 1) * 128],
                                ofull[:, pi * 8 + kc, tb * 512:(tb + 1) * 512],
                                start=(kc == 0), stop=(kc == 7)), rd=[wpb[s], ofb], wr=[pb[4 + pi * 2 + tb]])
                for gi in range(2):
                    for tb in range(2):
                        k.op(ACT, lambda gi=gi, tb=tb: nc.scalar.activation(sg[s][:, gi, tb, :], ps[gi * 2 + tb][:, :], AF.Sigmoid),
                             rd=[pb[gi * 2 + tb]], wr=[sgb[s]])
                for gi in range(2):
                    for tb in range(2):
                        k.op(DVE, lambda gi=gi, tb=tb: nc.vector.tensor_tensor(
                            sg[s][:, gi, tb, :], sg[s][:, gi, tb, :], ps[4 + gi * 2 + tb][:, :], ALU.mult),
                            rd=[sgb[s], pb[4 + gi * 2 + tb]], wr=[sgb[s]])
                k.op(POOL, lambda: nc.gpsimd.tensor_tensor(
                    mT[:, cc, :], sg[s][:, 0, :, :].rearrange("p a b -> p (a b)"),
                    sg[s][:, 1, :, :].rearrange("p a b -> p (a b)"), ALU.add), rd=[sgb[s]], wr=[mTb[cc]])
            tap('mT0', mT[:, 0, :], 128, TG)
            k.barrier()
        wo = [sb("wo%d" % i, [128, 16 * 512], BF16, st) for i in range(2)]
        wob = [Buf() for _ in range(2)]
        xb_ = [sb("xb%d" % i, [128, 512], F32, st) for i in range(4)]
        xbb = [Buf() for _ in range(4)]
        ybuf = sb("ybuf", [128, 8, D], F32, st)
        ybb = [Buf() for _ in range(8)]
        junk = sb("junk", [128, D], BF16, st)
        jb = Buf()
        sts = sb("sts", [128, 16], F32, st)
        stb = [Buf() for _ in range(8)]
        outdeps = []
        xi = 0
        for nb in range(4):
            wi = nb % 2
            k.dma(POOL, wo[wi][:], wout_d[nb, :, :], wr=[wob[wi]], max_dma_last_dim=8192)
            for tt in range(8):
                bk = tt
                row0 = tt * 128
                xs_ = xi % 4
                xi += 1
                k.dma(SP, xb_[xs_][:], xtok[row0:row0 + 128, nb * 512:(nb + 1) * 512], wr=[xbb[xs_]])
                for kc in range(16):
                    k.op(PE, lambda kc=kc, tt=tt, bk=bk: nc.tensor.matmul(
                        ps[bk][:, :], mT[:, kc, tt * 128:(tt + 1) * 128], wo[wi][:, kc * 512:(kc + 1) * 512],
                        start=(kc == 0), stop=(kc == 15)), rd=[mTb[kc], wob[wi]], wr=[pb[bk]])
                k.op(DVE, lambda tt=tt, bk=bk, xs_=xs_: nc.vector.tensor_tensor(
                    ybuf[:, tt, nb * 512:(nb + 1) * 512], ps[bk][:, :], xb_[xs_][:], ALU.add),
                    rd=[pb[bk], xbb[xs_]], wr=[ybb[tt]])
        tap('y0', ybuf[:, 0, :])
        for tt in range(8):
            row0 = tt * 128
            k.op(ACT, lambda tt=tt: nc.scalar.activation(junk[:], ybuf[:, tt, :], AF.Square, accum_out=sts[:, tt:tt + 1]),
                 rd=[ybb[tt]], wr=[jb, stb[tt]])
            k.op(ACT, lambda tt=tt: nc.scalar.activation(sts[:, 8 + tt:9 + tt], sts[:, tt:tt + 1], AF.Sqrt,
                                                         bias=RMS_EPS, scale=1.0 / D), rd=[stb[tt]], wr=[stb[tt]])
            k.op(DVE, lambda tt=tt: nc.vector.reciprocal(sts[:, 8 + tt:9 + tt], sts[:, 8 + tt:9 + tt]),
                 rd=[stb[tt]], wr=[stb[tt]])
            k.op(DVE, lambda tt=tt: nc.vector.scalar_tensor_tensor(
                ybuf[:, tt, :], ybuf[:, tt, :], sts[:, 8 + tt:9 + tt], fg[:], ALU.mult, ALU.mult),
                rd=[ybb[tt], stb[tt], fgbuf], wr=[ybb[tt]])
            outdeps.append(k.dma(SP, out_d[row0:row0 + 128, :], ybuf[:, tt, :], rd=[ybb[tt]]))
        for d in outdeps:
            k.wait(SP, d)
    k.barrier(dma_queues=("sp", "pool"))


_CACHE = {}


def _consts():
    c = np.zeros((128, 2304), np.float32)
    p = np.arange(128)
    a = p % 64
    b64 = np.arange(64)
    c[:, 0:128] = np.eye(128)
    tri = (p[:, None] <= p[None, :]).astype(np.float32)
    c[:, 128:256] = tri
    c[:, 256:384] = 1.0
    c[:, 384:512] = (p[:, None] // 64 == p[None, :] // 64)
    c[:, 512:576] = (a[:, None] == b64[None, :])
    lt = (a[:, None] < b64[None, :]).astype(np.float32)
    gt = (b64[None, :] < a[:, None]).astype(np.float32)
    le = (a[:, None] <= b64[None, :]).astype(np.float32)
    c[:, 576:1088] = np.tile(np.concatenate([lt, gt], 1), (1, 4))
    c[:, 1088:1472] = np.tile(np.concatenate([lt, le, le], 1), (1, 2))
    c[:, 1920:2048] = tri
    c[:, 2048:2176] = 1.0
    rm = np.ones((128, TH), np.float32)
    rm[:, ::CH] = 0.0
    return c, rm


def _tile_fm(W, cols):
    return np.ascontiguousarray(W[:, cols].reshape(16, 128, len(cols)).transpose(1, 0, 2))


def kernel(x, norm_gain, w_in, fox_forget_bias, rwkv_shift_mix, rwkv_w0, rwkv_w2, rwkv_a0, rwkv_a2,
           rwkv_k_k, rwkv_k_a, rwkv_r_k, rwkv_ln_w, rwkv_ln_b, w_proj_fox, w_proj_rwkv, w_out,
           final_norm_gain):
    f = lambda a: np.asarray(a, dtype=np.float32)
    x = f(x)
    W = f(w_in)[0]
    g = f(norm_gain)[0]
    mu = f(rwkv_shift_mix)[0]
    w0, a0, kk_, ka_ = f(rwkv_w0)[0], f(rwkv_a0)[0], f(rwkv_k_k)[0], f(rwkv_k_a)[0]
    rk_ = f(rwkv_r_k)[0].reshape(-1)
    lnw, lnb = f(rwkv_ln_w)[0], f(rwkv_ln_b)[0]
    w2, a2 = f(rwkv_w2)[0], f(rwkv_a2)[0]
    wpf, wpr, wo = f(w_proj_fox)[0], f(w_proj_rwkv)[0], f(w_out)[0]
    fg = f(final_norm_gain)
    fbv = f(fox_forget_bias)[0]
    cst, rmask = _consts()
    ar = np.arange

    wpf_t = np.ascontiguousarray(wpf.reshape(8, 128, 16, 128).transpose(2, 1, 0, 3)).reshape(16, 128, 1024)
    wpr_t = np.ascontiguousarray(wpr.reshape(8, 128, 16, 128).transpose(2, 1, 0, 3)).reshape(16, 128, 1024)
    wo_t = np.ascontiguousarray(wo.reshape(16, 128, 4, 512).transpose(2, 1, 0, 3)).reshape(4, 128, 16 * 512)
    fgb = np.ascontiguousarray(np.broadcast_to(fg[None, :], (128, D)))
    gate_chunks = [_tile_fm(W, G0 + ar(c * 128, (c + 1) * 128)).reshape(128, 2048) for c in range(32)]

    in_maps = []
    for c in range(NCORE):
        b, hf = c // 2, c % 2
        chunks = []
        for j in range(4):
            for base in (0, 1024, 3072):
                chunks.append(_tile_fm(W, base + 512 * hf + ar(128 * j, 128 * j + 128)).reshape(128, 2048))
        for j in range(4):
            for base in (0, 1024, 2048, 3072):
                chunks.append(_tile_fm(W, R0 + base + 512 * hf + ar(128 * j, 128 * j + 128)).reshape(128, 2048))
        wfm = np.stack(chunks + gate_chunks, 0)
        wlora = np.stack([_tile_fm(W, R0 + 4096 + li * 96 + ar(96)).reshape(128, 16 * 96) for li in range(2)], 0)
        vcols = np.concatenate([2048 + 512 * hf + ar(512), 4096 + 4 * hf + ar(4)])
        wv = _tile_fm(W, vcols).reshape(128, 16 * 516)
        pp = np.zeros((128, 64), np.float32)
        pp[:, 0:16] = g.reshape(16, 128).T
        for kind in range(4):
            for j in range(4):
                pp[:, 16 + kind * 4 + j] = mu[kind * 1024 + 512 * hf + 128 * j + ar(128)]
        for li in range(2):
            pp[0:96, 32 + li] = mu[4096 + li * 96 + ar(96)]
        for j in range(4):
            sl = 512 * hf + 128 * j + ar(128)
            pp[:, 34 + j] = w0[sl]
            pp[:, 38 + j] = a0[sl]
            pp[:, 42 + j] = kk_[sl]
            pp[:, 48 + j] = ka_[sl]
            pp[:, 52 + j] = rk_[sl]
        lnwb = np.zeros((4, 128, 64), np.float32)
        lnbb = np.zeros((4, 128, 64), np.float32)
        for j in range(4):
            for h in range(2):
                sl = 512 * hf + 128 * j + 64 * h + ar(64)
                lnwb[j, 64 * h:64 * h + 64, :] = lnw[sl][None, :]
                lnbb[j, 64 * h:64 * h + 64, :] = lnb[sl][None, :]
        sel = np.zeros((128, 2), np.float32)
        sel[:, hf] = 1.0
        xb = x[b]
        in_maps.append({
            "xT": np.ascontiguousarray(xb.T),
            "xTg": np.ascontiguousarray(xb[hf * 1024:(hf + 1) * 1024].T),
            "xtok": np.ascontiguousarray(xb[hf * 1024:(hf + 1) * 1024]),
            "wfm": wfm, "wlora": wlora, "wv": wv,
            "w2": np.ascontiguousarray(w2[:, 512 * hf:512 * hf + 512]),
            "a2": np.ascontiguousarray(a2[:, 512 * hf:512 * hf + 512]),
            "wpf": wpf_t, "wpr": wpr_t, "wout": wo_t, "pp": pp, "lnwb": lnwb, "lnbb": lnbb,
            "fgb": fgb, "fbias": np.ascontiguousarray(np.broadcast_to(fbv[4 * hf:4 * hf + 4][None, :], (128, 4))),
            "cst": cst, "rmask": rmask, "sel": sel,
        })
    if "nc" not in _CACHE:
        _CACHE["nc"] = build()
    res = run_bass_kernel_spmd(_CACHE["nc"], in_maps, core_ids=list(range(NCORE)))
    out = np.zeros((4, T, D), np.float32)
    for c in range(NCORE):
        b, hf = c // 2, c % 2
        out[b, hf * 1024:(hf + 1) * 1024] = np.asarray(res.results[c]["out"])
    return out
```

```python
import os
import numpy as np
import concourse.bass as bass
import concourse.mybir as mybir
from concourse.bass_utils import run_bass_kernel_spmd
from contextlib import ExitStack

F32 = mybir.dt.float32
BF16 = mybir.dt.bfloat16
AF = mybir.ActivationFunctionType
ALU = mybir.AluOpType
AX = mybir.AxisListType

D = 2048
T = 2048
NCORE = 8
R0 = 4104
G0 = 4104 + 4288
NFC = 60
CH = 64
NCH = T // CH
TH = T // 2
NCHH = TH // CH
RMS_EPS = 1e-6
GN_EPS = 64e-5
DEBUG = os.environ.get("KDEBUG", "")
KSTOP = int(os.environ.get("KSTOP", "99"))
KITER = int(os.environ.get("KITER", "1"))


class _Stop(Exception):
    pass


class Buf:
    __slots__ = ("w", "r")

    def __init__(self):
        self.w = None
        self.r = {}


class Eng:
    def __init__(self, name, h, sem, same):
        self.name = name
        self.h = h
        self.sem = sem
        self.cnt = 0
        self.waited = {}
        self.same = same
        self.ring = []
        self.dma_i = 0


class K:
    def __init__(self, nc, es):
        self.nc = nc
        mk = lambda n: es.enter_context(nc.semaphore(n))
        self.pe = Eng("pe", nc.tensor, mk("s_pe"), False)
        self.act = Eng("act", nc.scalar, mk("s_act"), True)
        self.dve = Eng("dve", nc.vector, mk("s_dve"), True)
        self.pool = Eng("pool", nc.gpsimd, mk("s_pool"), True)
        self.sp = Eng("sp", nc.sync, mk("s_sp"), False)
        self.engs = [self.pe, self.act, self.dve, self.pool, self.sp]
        self.log = {e.name: [] for e in self.engs}
        self.sp.ring = [[mk("d_sp%d" % i), 0] for i in range(20)]
        self.pool.ring = [[mk("d_pl%d" % i), 0] for i in range(12)]

    def wait(self, E, dep):
        key, sem, val = dep
        if E.waited.get(key, 0) >= val:
            return
        E.h.wait_ge(sem, val)
        self.log[E.name].append(('w', id(sem), val))
        E.waited[key] = val

    def _deps(self, E, rd, wr, name=None):
        name = E.name if name is None else name
        for b in rd:
            if b.w is not None:
                if b.w[0] == name:
                    if E.same:
                        self.wait(E, b.w)
                else:
                    self.wait(E, b.w)
        for b in wr:
            if b.w is not None and b.w[0] != name:
                self.wait(E, b.w)
            for d in b.r.values():
                if d[0] != name:
                    self.wait(E, d)

    def _mark(self, dep, rd, wr):
        for b in rd:
            b.r[dep[0]] = dep
        for b in wr:
            b.w = dep
            b.r = {}

    def op(self, E, fn, rd=(), wr=()):
        self._deps(E, rd, wr)
        inst = fn()
        inst.then_inc(E.sem, 1)
        self.log[E.name].append(('i', id(E.sem), 1))
        E.cnt += 1
        self._mark((E.name, E.sem, E.cnt), rd, wr)

    def dma(self, Q, out, in_, rd=(), wr=(), **kw):
        slot = Q.dma_i % len(Q.ring)
        Q.dma_i += 1
        sem, issued = Q.ring[slot]
        key = ("dma", Q.name, slot)
        if issued > 0:
            self.wait(Q, (key, sem, issued * 16))
        self._deps(Q, rd, wr, name='__dma__')
        inst = Q.h.dma_start(out=out, in_=in_, **kw)
        inst.then_inc(sem, 16)
        self.log[Q.name].append(('i', id(sem), 16))
        Q.ring[slot][1] += 1
        dep = (key, sem, Q.ring[slot][1] * 16)
        self._mark(dep, rd, wr)
        return dep

    def barrier(self, dma_queues=("sp",)):
        for E in self.engs:
            for X in self.engs:
                if X is not E and X.cnt > 0:
                    self.wait(E, (X.name, X.sem, X.cnt))
            for Q in self.engs:
                if Q.name in dma_queues:
                    for slot, (sem, issued) in enumerate(Q.ring):
                        if issued > 0:
                            self.wait(E, (("dma", Q.name, slot), sem, issued * 16))


def build():
    nc = bass.Bass("TRN2", target_bir_lowering=False)
    es = ExitStack()
    k = K(nc, es)
    _CACHE['k'] = k
    try:
        _build_inner(nc, es, k)
        es.close()
    except _Stop:
        k.barrier(dma_queues=("sp", "pool"))
    return nc


def _build_inner(nc, es, k):
    PE, ACT, DVE, POOL, SP = k.pe, k.act, k.dve, k.pool, k.sp

    def din(name, shape, dt=F32):
        return nc.dram_tensor(name, list(shape), dt, kind="ExternalInput").ap()

    xT = din("xT", [D, T])
    xTg = din("xTg", [D, T // 2])
    xtok = din("xtok", [T // 2, D])
    wfm = din("wfm", [NFC, 128, 2048])
    wlora = din("wlora", [2, 128, 16 * 96])
    wv_d = din("wv", [128, 16 * 516])
    w2_d = din("w2", [96, 512])
    a2_d = din("a2", [96, 512])
    wpf_d = din("wpf", [16, 128, 1024])
    wpr_d = din("wpr", [16, 128, 1024])
    wout_d = din("wout", [4, 128, 16 * 512])
    pp_d = din("pp", [128, 64])
    lnwb_d = din("lnwb", [4, 128, 64])
    lnbb_d = din("lnbb", [4, 128, 64])
    fgb_d = din("fgb", [128, D])
    fbias_d = din("fbias", [128, 4])
    cst_d = din("cst", [128, 2304])
    rmask_d = din("rmask", [128, TH])
    sel_d = din("sel", [128, 2])
    out_d = nc.dram_tensor("out", [T // 2, D], F32, kind="ExternalOutput").ap()
    cinA = nc.dram_tensor("cinA", [512, T], BF16)
    coutA = nc.dram_tensor("coutA", [1024, T], BF16)
    cinB = nc.dram_tensor("cinB", [512, T], BF16)
    coutB = nc.dram_tensor("coutB", [1024, T], BF16)
    cin_v = [cinA.ap().rearrange("(c p) t -> c p t", c=4, p=128), cinB.ap().rearrange("(c p) t -> c p t", c=4, p=128)]
    RG = [[0, 1], [2, 3], [4, 5], [6, 7]]
    ccsA = es.enter_context(nc.semaphore("ccsA"))
    ccsB = es.enter_context(nc.semaphore("ccsB"))

    nctr = {"i": 0}

    def sb(name, shape, dt, st=es):
        nctr["i"] += 1
        return st.enter_context(nc.sbuf_tensor("s%d_%s" % (nctr["i"], name), list(shape), dt))

    ps = [es.enter_context(nc.psum_tensor("ps%d" % i, [128, 512], F32)) for i in range(8)]
    pb = [Buf() for _ in range(8)]
    ps7b = ps[7].bitcast(BF16)
    ps6b = ps[6].bitcast(BF16)
    dbg_d = nc.dram_tensor("dbg", [128, T], F32, kind="ExternalOutput").ap() if DEBUG else None

    def tap(name, ap, P=128, n=T, view=None):
        if DEBUG != name:
            return
        k.barrier(dma_queues=("sp", "pool"))
        o = dbg_d[0:P, 0:n]
        if view is not None:
            o = o.rearrange(view[0], **view[1])
        d = k.dma(POOL, o, ap)
        k.wait(POOL, d)
        raise _Stop()

    wbuf = [sb("wbuf%d" % i, [128, 2048], BF16) for i in range(4)]
    wbb = [Buf() for _ in range(4)]
    pp = sb("pp", [128, 64], F32)
    omka = sb("omka", [128, 4], F32)
    omu = sb("omu", [128, 18], F32)
    sel = sb("sel", [128, 2], F32)
    cident = sb("cident", [128, 128], BF16)
    ctri = sb("ctri", [128, 128], BF16)
    cones = sb("cones", [128, 128], BF16)
    cbones = sb("cbones", [128, 128], BF16)
    cistack = sb("cistack", [128, 64], BF16)
    cmnl = sb("cmnl", [128, 512], BF16)
    cma3 = sb("cma3", [128, 384], BF16)
    ctriu32 = sb("ctriu32", [128, 128], F32)
    cones32 = sb("cones32", [128, 128], F32)
    cb = Buf()

    k.dma(SP, pp[:], pp_d[:, :], wr=[cb])
    k.dma(SP, sel[:], sel_d[:, :], wr=[cb])
    k.dma(SP, ctriu32[:], cst_d[:, 1920:2048], wr=[cb])
    k.dma(SP, cones32[:], cst_d[:, 2048:2176], wr=[cb])
    k.dma(POOL, cident[:], cst_d[:, 0:128], wr=[cb])
    k.dma(POOL, ctri[:], cst_d[:, 128:256], wr=[cb])
    k.dma(POOL, cones[:], cst_d[:, 256:384], wr=[cb])
    k.dma(POOL, cbones[:], cst_d[:, 384:512], wr=[cb])
    k.dma(POOL, cistack[:], cst_d[:, 512:576], wr=[cb])
    k.dma(POOL, cmnl[:], cst_d[:, 576:1088], wr=[cb])
    k.dma(POOL, cma3[:], cst_d[:, 1088:1472], wr=[cb])
    PG, PMU, PMUL, PW0, PA0, PKK, PKA, PRK = 0, 16, 32, 34, 38, 42, 48, 52
    k.op(DVE, lambda: nc.vector.tensor_scalar(omka[:], pp[:, PKA:PKA + 4], -1.0, 1.0, ALU.mult, ALU.add),
         rd=[cb], wr=[cb])
    k.op(DVE, lambda: nc.vector.tensor_scalar(omu[:], pp[:, PMU:PMU + 18], -1.0, 1.0, ALU.mult, ALU.add),
         rd=[cb], wr=[cb])

    wseq = list(range(28)) + [x for cc in range(16) for x in (28 + cc, 44 + cc)]
    wstate = {"next": 0}

    def wprefetch(upto):
        while wstate["next"] <= min(upto, len(wseq) - 1):
            i = wstate["next"]
            k.dma(POOL, wbuf[i % 4][:], wfm[wseq[i], :, :], wr=[wbb[i % 4]])
            wstate["next"] += 1

    pset = {"i": 0}

    def proj_fm(fc, hsrc, ntb=4, tb0=0, pf=2):
        wprefetch(fc + pf)
        base = 4 * (pset["i"] % 2)
        pset["i"] += 1
        w = wbuf[fc % 4]
        for dc in range(16):
            for tb in range(ntb):
                bk = base + tb
                k.op(PE, lambda bk=bk, dc=dc, tb=tb: nc.tensor.matmul(
                    ps[bk][:, :], w[:, dc * 128:(dc + 1) * 128],
                    hsrc[:, dc, (tb0 + tb) * 512:(tb0 + tb + 1) * 512],
                    start=(dc == 0), stop=(dc == 15)),
                    rd=[wbb[fc % 4]], wr=[pb[bk]])
        return [base + tb for tb in range(ntb)]

    def make_hT(dst, src_d, ntb, st):
        xs = [sb("xs%d" % i, [128, 16, 512], F32, st) for i in range(2)]
        xsb = [[Buf() for _ in range(16)] for _ in range(2)]
        sq = [sb("sq%d" % i, [128, 512], BF16, st) for i in range(3)]
        sqb = [Buf() for _ in range(3)]
        rs_ = sb("rs_", [128, 512], F32, st)
        rsb = Buf()
        for tb in range(ntb):
            X = xs[tb % 2]
            Xb = xsb[tb % 2]
            for dc in range(16):
                k.dma(SP, X[:, dc, :], src_d[dc * 128:(dc + 1) * 128, tb * 512:(tb + 1) * 512], wr=[Xb[dc]])
            bk = tb % 2
            for dc in range(16):
                s = (tb * 16 + dc) % 3
                k.op(ACT, lambda dc=dc, s=s: nc.scalar.activation(sq[s][:], X[:, dc, :], AF.Square),
                     rd=[Xb[dc]], wr=[sqb[s]])
                k.op(PE, lambda dc=dc, s=s: nc.tensor.matmul(ps[bk][:, :], cones[:], sq[s][:],
                                                             start=(dc == 0), stop=(dc == 15)),
                     rd=[sqb[s], cb], wr=[pb[bk]])
            k.op(ACT, lambda: nc.scalar.activation(rs_[:], ps[bk][:, :], AF.Sqrt, bias=RMS_EPS, scale=1.0 / D),
                 rd=[pb[bk]], wr=[rsb])
            k.op(DVE, lambda: nc.vector.reciprocal(rs_[:], rs_[:]), rd=[rsb], wr=[rsb])
            for dc in range(16):
                k.op(DVE, lambda dc=dc: nc.vector.scalar_tensor_tensor(
                    dst[:, dc, tb * 512:(tb + 1) * 512], X[:, dc, :], pp[:, PG + dc:PG + dc + 1], rs_[:],
                    ALU.mult, ALU.mult), rd=[Xb[dc], rsb, cb], wr=[])

    wprefetch(1)
    with ExitStack() as sm:
        hT = sb("hT", [128, 16, T], BF16, sm)
        rawr = [sb("rawr%d" % i, [128, 513], F32, sm) for i in range(2)]
        rawb = [Buf() for _ in range(2)]
        carry = sb("carry", [128, 8], F32, sm)
        carb = Buf()
        twd = sb("twd", [96, T], BF16, sm)
        adb = sb("adb", [96, T], BF16, sm)
        w2b = sb("w2b", [96, 512], BF16, sm)
        a2b = sb("a2b", [96, 512], BF16, sm)
        lb = Buf()
        k.dma(POOL, w2b[:], w2_d[:, :], wr=[lb])
        k.dma(POOL, a2b[:], a2_d[:, :], wr=[lb])
        sfox = ExitStack()
        wv = sb("wvs", [128, 16 * 516], BF16, sfox)
        wvb = Buf()
        k.dma(POOL, wv[:], wv_d[:, :], wr=[wvb], max_dma_last_dim=8192)
        with ExitStack() as st:
            make_hT(hT, xT, 4, st)
            k.barrier()
            tap('hT0', hT[:, 0, :])
            tap('hT15', hT[:, 15, :])
            if KSTOP <= 1:
                raise _Stop()

        rstate = {"i": 0}

        def shift_blocks(P, banks, dst, dstb, mucol, ccol, first):
            for bi, bk in enumerate(banks):
                R = rawr[rstate["i"] % 2]
                Rb = rawb[rstate["i"] % 2]
                rstate["i"] += 1
                if bi == 0:
                    if first:
                        k.op(DVE, lambda R=R: nc.vector.memset(R[0:P, 0:1], 0.0), wr=[Rb])
                    else:
                        k.op(DVE, lambda R=R: nc.vector.tensor_copy(R[0:P, 0:1], carry[0:P, ccol:ccol + 1]),
                             rd=[carb], wr=[Rb])
                else:
                    k.op(DVE, lambda R=R, Rp=Rp: nc.vector.tensor_copy(R[0:P, 0:1], Rp[0:P, 512:513]),
                         rd=[Rpb], wr=[Rb])
                k.op(ACT, lambda R=R, bk=bk: nc.scalar.copy(R[0:P, 1:513], ps[bk][0:P, :]), rd=[pb[bk]], wr=[Rb])
                sl = slice(bi * 512, (bi + 1) * 512)
                k.op(DVE, lambda R=R, sl=sl: nc.vector.tensor_scalar(
                    dst[0:P, sl], R[0:P, 1:513], omu[0:P, mucol:mucol + 1], None, ALU.mult), rd=[Rb, cb], wr=[dstb])
                k.op(DVE, lambda R=R, sl=sl: nc.vector.scalar_tensor_tensor(
                    dst[0:P, sl], R[0:P, 0:512], pp[0:P, PMU + mucol:PMU + mucol + 1], dst[0:P, sl],
                    ALU.mult, ALU.add), rd=[Rb, dstb, cb], wr=[dstb])
                Rp, Rpb = R, Rb
            k.op(DVE, lambda: nc.vector.tensor_copy(carry[0:P, ccol:ccol + 1], Rp[0:P, 512:513]),
                 rd=[Rpb], wr=[carb])

        with ExitStack() as st:
            wl = sb("wl", [128, 16 * 96], BF16, st)
            wlb = Buf()
            shf = sb("lshf", [96, T], F32, st)
            shfb = Buf()
            for li in range(2):
                k.dma(POOL, wl[:], wlora[li, :, :], wr=[wlb])
                for dc in range(16):
                    for tb in range(4):
                        k.op(PE, lambda dc=dc, tb=tb, li=li: nc.tensor.matmul(
                            ps[4 * li + tb][0:96, :], wl[:, dc * 96:(dc + 1) * 96], hT[:, dc, tb * 512:(tb + 1) * 512],
                            start=(dc == 0), stop=(dc == 15)), rd=[wlb], wr=[pb[4 * li + tb]])
                shift_blocks(96, [4 * li + b_ for b_ in range(4)], shf, shfb, 16 + li, 4 + li, True)
                if li == 0:
                    k.op(ACT, lambda: nc.scalar.activation(twd[:], shf[:], AF.Tanh), rd=[shfb], wr=[lb])
                else:
                    k.op(ACT, lambda: nc.scalar.copy(adb[:], shf[:]), rd=[shfb], wr=[lb])
            k.barrier()
            tap('twd', twd[:], 96)
            tap('adb', adb[:], 96)
            if KSTOP <= 2:
                raise _Stop()

        with ExitStack() as st:
            V = sb("V", [128, 16, 4, 132], BF16, st)
            Vb = Buf()
            fbias = sb("fbias", [128, 4], F32, st)
            k.dma(SP, fbias[:], fbias_d[:, :], wr=[Vb])
            xall = sb("xall", [128, 16, 4], F32, st)
            spall = sb("spall", [128, 16, 4], F32, st)
            xab = Buf()
            k.op(POOL, lambda: nc.gpsimd.memset(V[:], 1.0), wr=[Vb])
            for tt in range(16):
                bv = tt % 4
                bf = 4 + tt % 2
                for dc in range(16):
                    k.op(PE, lambda dc=dc: nc.tensor.matmul(
                        ps[bv][:, :], hT[:, dc, tt * 128:(tt + 1) * 128], wv[:, dc * 516:dc * 516 + 512],
                        start=(dc == 0), stop=(dc == 15)), rd=[wvb], wr=[pb[bv]])
                for dc in range(16):
                    k.op(PE, lambda dc=dc: nc.tensor.matmul(
                        ps[bf][:, 0:4], hT[:, dc, tt * 128:(tt + 1) * 128], wv[:, dc * 516 + 512:dc * 516 + 516],
                        start=(dc == 0), stop=(dc == 15)), rd=[wvb], wr=[pb[bf]])
                k.op(ACT, lambda: nc.scalar.copy(V[:, tt, :, 0:128], ps[bv][:, :].rearrange("p (h d) -> p h d", h=4)),
                     rd=[pb[bv]], wr=[Vb])
                k.op(DVE, lambda: nc.vector.tensor_tensor(xall[:, tt, :], ps[bf][:, 0:4], fbias[:], ALU.add),
                     rd=[pb[bf], Vb], wr=[xab])
            k.op(ACT, lambda: nc.scalar.activation(spall[:], xall[:], AF.Exp, scale=-1.0), rd=[xab], wr=[xab])
            k.op(ACT, lambda: nc.scalar.activation(spall[:], spall[:], AF.Ln, bias=1.0), rd=[xab], wr=[xab])
            spf = spall[:].rearrange("p a b -> p (a b)")
            sph = [sb("sph%d" % i, [128, 64], BF16, st) for i in range(3)]
            spr = sb("spr", [128, 64], F32, st)
            spt = sb("spt", [128, 64], F32, st)
            k.op(DVE, lambda: nc.vector.tensor_copy(sph[0][:], spf), rd=[xab], wr=[xab])
            k.op(DVE, lambda: nc.vector.tensor_copy(spt[:], sph[0][:]), rd=[xab], wr=[xab])
            k.op(DVE, lambda: nc.vector.tensor_tensor(spr[:], spf, spt[:], ALU.subtract), rd=[xab], wr=[xab])
            k.op(DVE, lambda: nc.vector.tensor_copy(sph[1][:], spr[:]), rd=[xab], wr=[xab])
            k.op(DVE, lambda: nc.vector.tensor_copy(spt[:], sph[1][:]), rd=[xab], wr=[xab])
            k.op(DVE, lambda: nc.vector.tensor_tensor(spr[:], spr[:], spt[:], ALU.subtract), rd=[xab], wr=[xab])
            k.op(DVE, lambda: nc.vector.tensor_copy(sph[2][:], spr[:]), rd=[xab], wr=[xab])
            for i in range(3):
                k.op(PE, lambda i=i: nc.tensor.matmul(ps[5][:, 0:64], ctri[:], sph[i][:], start=(i == 0), stop=(i == 2)),
                     rd=[xab, cb], wr=[pb[5]])
            for i in range(3):
                k.op(PE, lambda i=i: nc.tensor.matmul(ps[6][:, 0:64], cones[:], sph[i][:], start=(i == 0), stop=(i == 2)),
                     rd=[xab, cb], wr=[pb[6]])
            pre = sb("pre", [128, 17, 4], F32, st)
            cinc = sb("cinc", [128, 16, 4], F32, st)
            tot = sb("tot", [128, 16, 4], F32, st)
            cbf = Buf()
            k.op(DVE, lambda: nc.vector.tensor_copy(tot[:].rearrange("p a b -> p (a b)"), ps[6][:, 0:64]),
                 rd=[pb[6]], wr=[cbf])
            k.op(DVE, lambda: nc.vector.memset(pre[:, 0, :], 0.0), wr=[cbf])
            for tt in range(16):
                k.op(DVE, lambda tt=tt: nc.vector.tensor_tensor(pre[:, tt + 1, :], pre[:, tt, :], tot[:, tt, :], ALU.add),
                     rd=[cbf], wr=[cbf])
            k.op(DVE, lambda: nc.vector.tensor_tensor(cinc[:].rearrange("p a b -> p (a b)"), ps[5][:, 0:64],
                                                      pre[:, 0:16, :].rearrange("p a b -> p (a b)"), ALU.add),
                 rd=[pb[5], cbf], wr=[cbf])
            fb = sb("fb", [128, 4, 8, 16], F32, st)
            for h in range(4):
                for hb in range(8):
                    k.op(DVE, lambda h=h, hb=hb: nc.vector.tensor_scalar(
                        fb[:, h, hb, :], cinc[:, :, h], pre[:, 2 * hb + 1, h:h + 1], None, ALU.subtract),
                        rd=[cbf], wr=[cbf])
            k.barrier()
            tap('V0', V[:, :, 0, 0:128], view=('p (a b) -> p a b', dict(a=16)))
            tap('cinc', cinc[:].rearrange('p a b -> p (a b)'), 128, 64)
            tap('fb', fb[:].rearrange('p a b c -> p (a b c)'), 128, 512)
            if KSTOP <= 3:
                raise _Stop()

            qT = sb("qT", [128, T], BF16, st)
            kT = sb("kT", [128, T], BF16, st)
            szaT = sb("szaT", [128, T], BF16, st)
            oaS = sb("oaS", [128, T], BF16, st)
            qb_, kb_, zb_, oab = Buf(), Buf(), Buf(), Buf()
            pT = [sb("pT%d" % i, [128, 512], BF16, st) for i in range(3)]
            pTb = [Buf() for _ in range(3)]
            onrm = [sb("onrm%d" % i, [128, 128], BF16, st) for i in range(4)]
            onb = [Buf() for _ in range(4)]
            rinv = [sb("rinv%d" % i, [128, 1], F32, st) for i in range(4)]
            pending = []

            def flush_T(upto):
                while pending and pending[0][0] <= upto:
                    _, tq_, s_, jj = pending.pop(0)
                    k.op(PE, lambda s_=s_: nc.tensor.transpose(ps7b[:, 0:128], onrm[s_][:], cident[:]),
                         rd=[onb[s_]], wr=[pb[7]])
                    k.op(DVE, lambda tq_=tq_: nc.vector.tensor_tensor(
                        oaS[:, tq_ * 128:(tq_ + 1) * 128], ps7b[:, 0:128], szaT[:, tq_ * 128:(tq_ + 1) * 128],
                        ALU.mult), rd=[pb[7], zb_], wr=[oab])
            scale = 128 ** -0.5
            for j in range(4):
                for kind, dst, dstb in ((0, qT, qb_), (1, kT, kb_), (2, szaT, zb_)):
                    banks = proj_fm(3 * j + kind, hT)
                    for tb, bk in enumerate(banks):
                        if kind == 2:
                            k.op(ACT, lambda tb=tb, bk=bk, dst=dst: nc.scalar.activation(
                                dst[:, tb * 512:(tb + 1) * 512], ps[bk][:, :], AF.Silu), rd=[pb[bk]], wr=[dstb])
                        elif tb % 2 == 0:
                            k.op(ACT, lambda tb=tb, bk=bk, dst=dst: nc.scalar.copy(
                                dst[:, tb * 512:(tb + 1) * 512], ps[bk][:, :]), rd=[pb[bk]], wr=[dstb])
                        else:
                            k.op(DVE, lambda tb=tb, bk=bk, dst=dst: nc.vector.tensor_copy(
                                dst[:, tb * 512:(tb + 1) * 512], ps[bk][:, :]), rd=[pb[bk]], wr=[dstb])
                steps = [(qb, kt) for qb in range(4) for kt in range(4 * qb + 4)]

                def issue_S(i):
                    qb, kt = steps[i]
                    jd = max(0, kt - 4 * qb)
                    q0 = qb * 512 + jd * 128
                    n = 512 - jd * 128
                    bk = i % 3
                    k.op(PE, lambda: nc.tensor.matmul(ps[bk][:, 0:n], kT[:, kt * 128:(kt + 1) * 128], qT[:, q0:q0 + n],
                                                      start=True, stop=True), rd=[kb_, qb_], wr=[pb[bk]])

                def issue_rest(i):
                    qb, kt = steps[i]
                    jd = max(0, kt - 4 * qb)
                    n = 512 - jd * 128
                    bk = i % 3
                    P_ = pT[i % 3]
                    Pb = pTb[i % 3]
                    for pr in range(2):
                        qis = [qi for qi in (2 * pr, 2 * pr + 1) if qi >= jd]
                        if not qis:
                            continue
                        cs = slice((qis[0] - jd) * 128, (qis[-1] - jd + 1) * 128)
                        k.op(ACT, lambda pr=pr, cs=cs: nc.scalar.activation(
                            P_[:, cs], ps[bk][:, cs], AF.Exp, bias=fb[:, j, 2 * qb + pr, kt:kt + 1], scale=scale),
                            rd=[pb[bk]], wr=[Pb])
                    if kt >= 4 * qb:
                        k.op(POOL, lambda: nc.gpsimd.tensor_tensor(P_[:, 0:128], P_[:, 0:128], ctri[:], ALU.mult),
                             rd=[Pb], wr=[Pb])
                    for qi in range(jd, 4):
                        ob = 3 + qi
                        last = (kt == 4 * qb + qi)
                        k.op(PE, lambda qi=qi, ob=ob, last=last: nc.tensor.matmul(
                            ps[ob][:, 0:129], P_[:, (qi - jd) * 128:(qi - jd + 1) * 128], V[:, kt, j, 0:129],
                            start=(kt == 0), stop=last), rd=[Pb], wr=[pb[ob]])
                        if last:
                            tq = 4 * qb + qi
                            s = tq % 4
                            k.op(DVE, lambda ob=ob, s=s: nc.vector.reciprocal(rinv[s][:], ps[ob][:, 128:129]),
                                 rd=[pb[ob]], wr=[onb[s]])
                            k.op(DVE, lambda ob=ob, s=s: nc.vector.tensor_scalar(
                                onrm[s][:], ps[ob][:, 0:128], rinv[s][:, 0:1], None, ALU.mult),
                                rd=[pb[ob], onb[s]], wr=[onb[s]])
                            pending.append((i, tq, s, j))

                n_st = len(steps)
                issue_S(0)
                issue_S(1)
                for i in range(n_st):
                    if i + 2 < n_st:
                        issue_S(i + 2)
                    flush_T(i - 2)
                    issue_rest(i)
                flush_T(n_st)
                k.dma(SP, cin_v[0][j], oaS[:], rd=[oab])
                tap('qT', qT[:])
                tap('kT', kT[:])
                tap('szaT', szaT[:])
                tap('oaS%d' % j, oaS[:])
                if KSTOP <= 4 and j + 1 >= KITER:
                    raise _Stop()

        k.barrier()
        sfox.close()
        nc.gpsimd.collective_compute("AllGather", ALU.bypass, replica_groups=RG,
                                     ins=[cinA.ap().opt()], outs=[coutA.ap().opt()]).then_inc(ccsA, 1)
        NEG_EH = -float(np.exp(-0.5))
        with ExitStack() as st:
            f32t = lambda n: sb(n, [128, TH], F32, st)
            bf16t = lambda n: sb(n, [128, TH], BF16, st)
            rmask = bf16t("rmask")
            rmb = Buf()
            k.dma(POOL, rmask[:], rmask_d[:, :], wr=[rmb])
            rs, ks, vs = f32t("rs"), f32t("ks"), f32t("vs")
            a_, g_, t1, t2 = f32t("a_"), f32t("g_"), f32t("t1"), f32t("t2")
            kk = vs
            szbT = bf16t("szbT")
            atT, rtT, ktT, btT = bf16t("atT"), bf16t("rtT"), bf16t("ktT"), bf16t("btT")
            khT, bhT, vbT, rkT = bf16t("khT"), bf16t("bhT"), bf16t("vbT"), bf16t("rkT")
            obS = bf16t("obS")
            gam = sb("gam", [128, NCHH], F32, st)
            KBV = sb("KBV", [128, NCHH, 3, 64], BF16, st)
            A3 = sb("A3", [128, NCHH, 3, 64], BF16, st)
            TT = sb("TT", [128, NCHH, 64], BF16, st)
            NLs = [[sb("NLs%d_%d" % (b4, i), [128, 4, 2, 64], BF16, st) for i in range(2)] for b4 in range(4)]
            NLsb = [[Buf(), Buf()] for _ in range(4)]
            Ts = [[sb("Ts%d_%d" % (b4, i), [128, 4, 64], BF16, st) for i in range(2)] for b4 in range(4)]
            Tsb = [[Buf(), Buf()] for _ in range(4)]
            prodb = [Buf() for _ in range(4)]
            S32 = sb("S32", [128, 64], F32, st)
            Sbf = [sb("Sbf%d" % i, [128, 64], BF16, st) for i in range(2)]
            Xsb = sb("Xsb", [128, 64], BF16, st)
            Usb = sb("Usb", [128, 64], BF16, st)
            lnw = sb("lnw", [128, 64], F32, st)
            lnb = sb("lnb", [128, 64], F32, st)
            yc = [sb("yc%d" % i, [128, 4, 64], F32, st) for i in range(2)]
            ysq = sb("ysq", [128, 4, 64], F32, st)
            ysqb = Buf()
            Gb = [sb("Gb%d" % i, [128, 4, 64], BF16, st) for i in range(2)]
            st1 = [sb("st1_%d" % i, [128, 16], F32, st) for i in range(2)]
            B = lambda: Buf()
            rsb, ksb, vsb, ab, gb, t1b, t2b, zbb = (B() for _ in range(8))
            kkb = vsb
            atb, rtb, ktb, btb, khb, bhb, vbb, rkb, gamb, obb = (B() for _ in range(10))
            KBVb, A3b, TTb = B(), B(), B()
            S32b, Sbfb, Xb, Ub, lnb_ = B(), [B(), B()], B(), B(), B()
            ycb, Gbb, st1b = [B(), B()], [B(), B()], [B(), B()]
            sc = 0

            def stage1(j, hh):
                fc0 = 12 + 4 * j
                for kind, dst, dstb in ((0, rs, rsb), (1, ks, ksb), (2, vs, vsb), (3, t2, t2b)):
                    fc = fc0 + kind
                    if hh == 0:
                        wprefetch(fc + 3 - kind)
                    w = wbuf[fc % 4]
                    n = 0
                    for dc in range(16):
                        for tb in range(2):
                            bk = 4 + tb
                            k.op(PE, lambda bk=bk, dc=dc, tb=tb, w=w: nc.tensor.matmul(
                                ps[bk][:, :], w[:, dc * 128:(dc + 1) * 128],
                                hT[:, dc, (2 * hh + tb) * 512:(2 * hh + tb + 1) * 512],
                                start=(dc == 0), stop=(dc == 15)), rd=[wbb[fc % 4]], wr=[pb[bk]])
                            n += 1
                            if n % 4 == 0:
                                yield
                    shift_blocks(128, [4, 5], dst, dstb, kind * 4 + j, kind, hh == 0)
                    yield
                if hh == 1:
                    wprefetch(fc0 + 3 + 3)

            its = [(j, hh) for j in range(4) for hh in range(2)]
            for _ in stage1(*its[0]):
                pass
            for it_i, (j, hh) in enumerate(its):
                if True:
                    if hh == 0:
                        k.dma(SP, lnw[:], lnwb_d[j, :, :], wr=[lnb_])
                        k.dma(SP, lnb[:], lnbb_d[j, :, :], wr=[lnb_])
                    t0 = hh * TH
                    nxt = stage1(*its[it_i + 1]) if it_i + 1 < len(its) else iter(())
                    k.op(ACT, lambda: nc.scalar.activation(szbT[:], t2[:], AF.Silu), rd=[t2b], wr=[zbb])
                    k.op(ACT, lambda: nc.scalar.copy(vbT[:], vs[:]), rd=[vsb], wr=[vbb])
                    for tb in range(2):
                        tsl = slice(t0 + tb * 512, t0 + (tb + 1) * 512)
                        k.op(PE, lambda tb=tb, tsl=tsl: nc.tensor.matmul(ps[tb][:, :], w2b[:, j * 128:(j + 1) * 128],
                                                                         twd[:, tsl], start=True, stop=True),
                             rd=[lb], wr=[pb[tb]])
                        k.op(PE, lambda tb=tb, tsl=tsl: nc.tensor.matmul(ps[4 + tb][:, :], a2b[:, j * 128:(j + 1) * 128],
                                                                         adb[:, tsl], start=True, stop=True),
                             rd=[lb], wr=[pb[4 + tb]])
                    for tb in range(2):
                        k.op(ACT, lambda tb=tb: nc.scalar.activation(t2[:, tb * 512:(tb + 1) * 512], ps[tb][:, :], AF.Sigmoid,
                                                                     bias=pp[:, PW0 + j:PW0 + j + 1]), rd=[pb[tb], zbb], wr=[t2b])
                        k.op(ACT, lambda tb=tb: nc.scalar.activation(a_[:, tb * 512:(tb + 1) * 512], ps[4 + tb][:, :], AF.Sigmoid,
                                                                     bias=pp[:, PA0 + j:PA0 + j + 1]), rd=[pb[4 + tb]], wr=[ab])
                    k.op(DVE, lambda: nc.vector.tensor_scalar(t2[:], t2[:], NEG_EH, None, ALU.mult), rd=[t2b], wr=[t2b])
                    k.op(DVE, lambda: nc.vector.tensor_tensor_scan(g_[:], rmask[:], t2[:], 0.0, ALU.mult, ALU.add),
                         rd=[rmb, t2b], wr=[gb])
                    k.op(DVE, lambda: nc.vector.tensor_scalar(kk[:], ks[:], pp[:, PKK + j:PKK + j + 1], None, ALU.mult),
                         rd=[ksb, vbb], wr=[kkb])
                    k.op(ACT, lambda: nc.scalar.activation(khT[:], kk[:], AF.Square), rd=[kkb], wr=[khb])
                    for tb in range(2):
                        k.op(PE, lambda tb=tb: nc.tensor.matmul(ps[tb][:, :], cbones[:], khT[:, tb * 512:(tb + 1) * 512],
                                                                start=True, stop=True), rd=[khb, cb], wr=[pb[tb]])
                        k.op(ACT, lambda tb=tb: nc.scalar.activation(t1[:, tb * 512:(tb + 1) * 512], ps[tb][:, :], AF.Sqrt),
                             rd=[pb[tb]], wr=[t1b])
                    k.op(DVE, lambda: nc.vector.tensor_scalar(t1[:], t1[:], 1e-12, None, ALU.max), rd=[t1b], wr=[t1b])
                    k.op(DVE, lambda: nc.vector.reciprocal(t1[:], t1[:]), rd=[t1b], wr=[t1b])
                    k.op(DVE, lambda: nc.vector.tensor_tensor(kk[:], kk[:], t1[:], ALU.mult), rd=[kkb, t1b], wr=[kkb])
                    k.op(DVE, lambda: nc.vector.tensor_scalar(t1[:], a_[:], pp[:, PKA + j:PKA + j + 1], omka[:, j:j + 1],
                                                              ALU.mult, ALU.add), rd=[ab, t1b, kkb], wr=[t1b])
                    k.op(DVE, lambda: nc.vector.tensor_tensor(ks[:], ks[:], t1[:], ALU.mult), rd=[ksb, t1b], wr=[ksb])
                    k.op(POOL, lambda: nc.gpsimd.tensor_tensor(a_[:], a_[:], kk[:], ALU.mult), rd=[ab, kkb, t1b], wr=[ab])
                    k.op(DVE, lambda: nc.vector.scalar_tensor_tensor(rkT[:], rs[:], pp[:, PRK + j:PRK + j + 1], ks[:],
                                                                     ALU.mult, ALU.mult), rd=[rsb, ksb], wr=[rkb])
                    k.op(ACT, lambda: nc.scalar.activation(t1[:], g_[:], AF.Exp), rd=[gb, ksb], wr=[t1b])
                    k.op(DVE, lambda: nc.vector.tensor_tensor(rtT[:], rs[:], t1[:], ALU.mult), rd=[rsb, t1b], wr=[rtb])
                    k.op(DVE, lambda: nc.vector.tensor_copy(gam[:], t1[:].rearrange("p (c i) -> p c i", i=CH)[:, :, CH - 1]),
                         rd=[t1b], wr=[gamb])
                    k.op(POOL, lambda: nc.gpsimd.tensor_tensor(t2[:], g_[:], t2[:], ALU.subtract), rd=[gb, t2b], wr=[t2b])
                    k.op(ACT, lambda: nc.scalar.activation(t1[:], t2[:], AF.Exp), rd=[t2b, rtb, gamb], wr=[t1b])
                    k.op(DVE, lambda: nc.vector.scalar_tensor_tensor(atT[:], kk[:], -1.0, t1[:], ALU.mult, ALU.mult),
                         rd=[kkb, t1b], wr=[atb])
                    k.op(ACT, lambda: nc.scalar.activation(t1[:], g_[:], AF.Exp, scale=-1.0), rd=[gb, atb], wr=[t1b])
                    k.op(DVE, lambda: nc.vector.tensor_tensor(ktT[:], ks[:], t1[:], ALU.mult), rd=[ksb, t1b], wr=[ktb])
                    k.op(POOL, lambda: nc.gpsimd.tensor_tensor(btT[:], a_[:], t1[:], ALU.mult), rd=[ab, t1b], wr=[btb])
                    g3 = g_[:].rearrange("p (c i) -> p c i", i=CH)
                    k.op(DVE, lambda: nc.vector.tensor_tensor(
                        t2[:].rearrange("p (c i) -> p c i", i=CH), g3[:, :, CH - 1:CH].broadcast_to([128, NCHH, CH]), g3,
                        ALU.subtract), rd=[gb, t2b], wr=[t2b])
                    k.op(ACT, lambda: nc.scalar.activation(t2[:], t2[:], AF.Exp), rd=[t2b], wr=[t2b])
                    k.op(DVE, lambda: nc.vector.tensor_tensor(khT[:], ks[:], t2[:], ALU.mult), rd=[ksb, t2b], wr=[khb])
                    k.op(POOL, lambda: nc.gpsimd.tensor_tensor(bhT[:], a_[:], t2[:], ALU.mult), rd=[ab, t2b], wr=[bhb])

                    for c0 in range(0, NCHH, 4):
                        tbank = 7 - (c0 // 4) % 2
                        tview = ps7b if tbank == 7 else ps6b
                        for ci in range(4):
                            c = c0 + ci
                            for qi, (src, srcb) in enumerate(((khT, khb), (bhT, bhb), (vbT, vbb))):
                                for h in range(2):
                                    col = (ci * 3 + qi) * 64
                                    k.op(PE, lambda h=h, c=c, col=col, src=src, tview=tview: nc.tensor.transpose(
                                        tview[64 * h:64 * h + 64, col:col + 64], src[64 * h:64 * h + 64, c * CH:(c + 1) * CH],
                                        cident[64 * h:64 * h + 64, 64 * h:64 * h + 64]), rd=[srcb, cb], wr=[pb[tbank]])
                        k.op(DVE, lambda c0=c0, tview=tview: nc.vector.tensor_copy(
                            KBV[:, c0:c0 + 4, :, :].rearrange("p a b c -> p (a b c)"), tview[:, 0:768]),
                            rd=[pb[tbank]], wr=[KBVb])

                    for b4 in range(4):
                        c0 = 4 * b4
                        for ci in range(4):
                            c = c0 + ci
                            tk = slice(c * CH, (c + 1) * CH)
                            for h in range(2):
                                hp = slice(64 * h, 64 * h + 64)
                                k.op(PE, lambda hp=hp, tk=tk, ci=ci: nc.tensor.matmul(
                                    ps[b4][hp, (ci * 2) * 64:(ci * 2 + 1) * 64], btT[hp, tk], atT[hp, tk],
                                    start=True, stop=True), rd=[btb, atb], wr=[pb[b4]])
                                k.op(PE, lambda hp=hp, tk=tk, ci=ci: nc.tensor.matmul(
                                    ps[b4][hp, (ci * 2 + 1) * 64:(ci * 2 + 2) * 64], atT[hp, tk], btT[hp, tk],
                                    start=True, stop=True), rd=[btb, atb], wr=[pb[b4]])
                        k.op(DVE, lambda b4=b4: nc.vector.tensor_tensor(
                            NLs[b4][0][:].rearrange("p a b c -> p (a b c)"), ps[b4][:, :], cmnl[:], ALU.mult),
                            rd=[pb[b4], cb], wr=[NLsb[b4][0]])
                        for half in range(2):
                            ba = 4 + half
                            for cj in range(2):
                                c = c0 + half * 2 + cj
                                tk = slice(c * CH, (c + 1) * CH)
                                for h in range(2):
                                    hp = slice(64 * h, 64 * h + 64)
                                    for qi, (lt, ltb, rt, rtb_) in enumerate(((ktT, ktb, atT, atb), (ktT, ktb, rtT, rtb),
                                                                              (btT, btb, rtT, rtb))):
                                        col = (cj * 3 + qi) * 64
                                        k.op(PE, lambda hp=hp, tk=tk, col=col, lt=lt, rt=rt, ba=ba: nc.tensor.matmul(
                                            ps[ba][hp, col:col + 64], lt[hp, tk], rt[hp, tk], start=True, stop=True),
                                            rd=[ltb, rtb_], wr=[pb[ba]])
                            cc0 = c0 + half * 2
                            k.op(DVE, lambda cc0=cc0, ba=ba: nc.vector.tensor_tensor(
                                A3[:, cc0:cc0 + 2, :, :].rearrange("p a b c -> p (a b c)"), ps[ba][:, 0:384], cma3[:], ALU.mult),
                                rd=[pb[ba], cb], wr=[A3b])
                        k.op(POOL, lambda b4=b4: nc.gpsimd.tensor_tensor(
                            Ts[b4][0][:], NLs[b4][0][:, :, 0, :],
                            cistack[:].rearrange("p (o c) -> p o c", o=1).broadcast_to([128, 4, 64]),
                            ALU.add), rd=[NLsb[b4][0], cb], wr=[Tsb[b4][0]])
                    cu = 0
                    for lev in range(1, 6):
                        if lev > int(os.environ.get('KLEV', '9')):
                            break
                        nx = 1 - cu
                        for b4 in range(4):
                            cur = NLs[b4][cu]
                            for ci in range(4):
                                for h in range(2):
                                    hp = slice(64 * h, 64 * h + 64)
                                    if lev < 5:
                                        k.op(PE, lambda hp=hp, ci=ci, cur=cur, b4=b4: nc.tensor.matmul(
                                            ps[b4][hp, (ci * 2) * 64:(ci * 2 + 1) * 64], cur[hp, ci, 1, :], cur[hp, ci, 0, :],
                                            start=True, stop=True), rd=[NLsb[b4][cu]], wr=[pb[b4]])
                                    k.op(PE, lambda hp=hp, ci=ci, cur=cur, b4=b4: nc.tensor.matmul(
                                        ps[b4][hp, (ci * 2 + 1) * 64:(ci * 2 + 2) * 64], cur[hp, ci, 0, :], cur[hp, ci, 1, :],
                                        start=True, stop=True), rd=[NLsb[b4][cu]], wr=[pb[b4]])
                            oth = NLs[b4][nx]
                            if lev < 5:
                                k.op(ACT, lambda oth=oth, b4=b4: nc.scalar.copy(oth[:].rearrange("p a b c -> p (a b c)"),
                                                                                ps[b4][:, :]), rd=[pb[b4]], wr=[NLsb[b4][nx]])
                            else:
                                k.op(ACT, lambda oth=oth, b4=b4: nc.scalar.copy(
                                    oth[:, :, 1, :], ps[b4][:, :].rearrange("p (a b c) -> p a b c", a=4, b=2)[:, :, 1, :]),
                                    rd=[pb[b4]], wr=[NLsb[b4][nx]])
                        for b4 in range(4):
                            new_ = NLs[b4][nx]
                            tc_ = Ts[b4][cu]
                            pbank = 4 + b4
                            pc0 = 0
                            for ci in range(4):
                                for h in range(2):
                                    hp = slice(64 * h, 64 * h + 64)
                                    k.op(PE, lambda hp=hp, ci=ci, new_=new_, tc_=tc_, pbank=pbank, pc0=pc0: nc.tensor.matmul(
                                        ps[pbank][hp, pc0 + ci * 64:pc0 + (ci + 1) * 64], new_[hp, ci, 1, :], tc_[hp, ci, :],
                                        start=True, stop=True), rd=[NLsb[b4][nx], Tsb[b4][cu]], wr=[pb[pbank], prodb[b4]])
                            if lev < 5:
                                k.op(DVE, lambda b4=b4, tc_=tc_, pbank=pbank, pc0=pc0: nc.vector.tensor_tensor(
                                    Ts[b4][nx][:].rearrange("p a c -> p (a c)"), ps[pbank][:, pc0:pc0 + 256],
                                    tc_[:].rearrange("p a c -> p (a c)"), ALU.add),
                                    rd=[prodb[b4], pb[pbank], Tsb[b4][cu]], wr=[Tsb[b4][nx]])
                            else:
                                k.op(DVE, lambda b4=b4, tc_=tc_, pbank=pbank, pc0=pc0: nc.vector.tensor_tensor(
                                    TT[:, 4 * b4:4 * b4 + 4, :].rearrange("p a c -> p (a c)"), ps[pbank][:, pc0:pc0 + 256],
                                    tc_[:].rearrange("p a c -> p (a c)"), ALU.add),
                                    rd=[prodb[b4], pb[pbank], Tsb[b4][cu]], wr=[TTb])
                        cu = nx
                    if KSTOP <= 5:
                        raise _Stop()

                    if hh == 0:
                        k.op(DVE, lambda: nc.vector.memset(S32[:], 0.0), wr=[S32b])
                        k.op(DVE, lambda: nc.vector.memset(Sbf[sc % 2][:], 0.0), wr=[Sbfb[sc % 2]])

                    def post_stage(stg, g):
                        c0 = 4 * g
                        p = g % 2
                        yb = 2 + p
                        Y3 = ps[yb][:, 0:256].rearrange("p (a b) -> p a b", a=4)
                        bc = lambda ap: ap.rearrange("p (a o) -> p a o", o=1).broadcast_to([128, 4, 64])
                        if stg == 1:
                            k.op(DVE, lambda: nc.vector.tensor_reduce(st1[p][:, 0:4], Y3, AX.X, ALU.add),
                                 rd=[pb[yb]], wr=[st1b[p]])
                            k.op(DVE, lambda: nc.vector.tensor_scalar(st1[p][:, 0:4], st1[p][:, 0:4], -1.0 / 64, None, ALU.mult),
                                 rd=[st1b[p]], wr=[st1b[p]])
                            k.op(DVE, lambda: nc.vector.tensor_tensor(yc[p][:], Y3, bc(st1[p][:, 0:4]), ALU.add),
                                 rd=[pb[yb], st1b[p]], wr=[ycb[p]])
                            k.op(DVE, lambda: nc.vector.tensor_copy(st1[p][:, 12:16], ps[yb][:, 256:260]),
                                 rd=[pb[yb]], wr=[st1b[p]])
                            k.op(POOL, lambda: nc.gpsimd.tensor_tensor(ysq[:], yc[p][:], yc[p][:], ALU.mult),
                                 rd=[ycb[p]], wr=[ysqb])
                        elif stg == 2:
                            k.op(DVE, lambda: nc.vector.tensor_reduce(st1[p][:, 4:8], ysq[:], AX.X, ALU.add),
                                 rd=[ysqb], wr=[st1b[p]])
                            k.op(ACT, lambda: nc.scalar.activation(st1[p][:, 8:12], st1[p][:, 4:8], AF.Sqrt, bias=GN_EPS,
                                                                   scale=1.0 / 64), rd=[st1b[p]], wr=[st1b[p]])
                        elif stg == 3:
                            k.op(DVE, lambda: nc.vector.reciprocal(st1[p][:, 8:12], st1[p][:, 8:12]), rd=[st1b[p]], wr=[st1b[p]])
                            k.op(DVE, lambda: nc.vector.tensor_tensor(yc[p][:], yc[p][:], bc(st1[p][:, 8:12]), ALU.mult),
                                 rd=[ycb[p], st1b[p]], wr=[ycb[p]])
                            k.op(DVE, lambda: nc.vector.tensor_tensor(
                                yc[p][:], yc[p][:], lnw[:].rearrange("p (o c) -> p o c", o=1).broadcast_to([128, 4, 64]), ALU.mult),
                                rd=[ycb[p], lnb_], wr=[ycb[p]])
                            k.op(POOL, lambda: nc.gpsimd.tensor_tensor(
                                yc[p][:], yc[p][:], lnb[:].rearrange("p (o c) -> p o c", o=1).broadcast_to([128, 4, 64]), ALU.add),
                                rd=[ycb[p], lnb_], wr=[ycb[p]])
                            k.op(POOL, lambda: nc.gpsimd.tensor_tensor(ysq[:], KBV[:, c0:c0 + 4, 2, :], bc(st1[p][:, 12:16]), ALU.mult),
                                 rd=[KBVb, st1b[p], ysqb], wr=[ysqb])
                        elif stg == 4:
                            k.op(DVE, lambda: nc.vector.tensor_tensor(Gb[p][:], ysq[:], yc[p][:], ALU.add),
                                 rd=[ysqb, ycb[p]], wr=[Gbb[p]])
                        elif stg == 5:
                            for ci in range(4):
                                for h in range(2):
                                    hp = slice(64 * h, 64 * h + 64)
                                    k.op(PE, lambda hp=hp, ci=ci: nc.tensor.transpose(
                                        ps7b[hp, ci * 64:(ci + 1) * 64], Gb[p][hp, ci, :], cident[hp, hp]),
                                        rd=[Gbb[p], cb], wr=[pb[7]])
                        else:
                            tks = slice(c0 * CH, (c0 + 4) * CH)
                            k.op(DVE, lambda: nc.vector.tensor_tensor(obS[:, tks], ps7b[:, 0:256], szbT[:, tks], ALU.mult),
                                 rd=[pb[7], zbb], wr=[obb])

                    for c in range(NCHH):
                        g, ci = c // 4, c % 4
                        tk = slice(c * CH, (c + 1) * CH)
                        Sc, Scb = Sbf[sc % 2], Sbfb[sc % 2]
                        Sn, Snb = Sbf[(sc + 1) % 2], Sbfb[(sc + 1) % 2]
                        yb = 2 + g % 2
                        ycol = slice(ci * 64, (ci + 1) * 64)
                        sc += 1
                        for h in range(2):
                            hp = slice(64 * h, 64 * h + 64)
                            k.op(PE, lambda hp=hp: nc.tensor.matmul(ps[0][hp, 0:64], A3[hp, c, 0, :], KBV[hp, c, 2, :],
                                                                    start=True, stop=False), rd=[A3b, KBVb], wr=[pb[0]])
                            k.op(PE, lambda hp=hp: nc.tensor.matmul(ps[0][hp, 0:64], atT[hp, tk], Sc[hp, :],
                                                                    start=False, stop=True), rd=[atb, Scb], wr=[pb[0]])
                        k.op(ACT, lambda: nc.scalar.copy(Xsb[:], ps[0][:, 0:64]), rd=[pb[0]], wr=[Xb])
                        for h in range(2):
                            hp = slice(64 * h, 64 * h + 64)
                            k.op(PE, lambda hp=hp: nc.tensor.matmul(ps[yb][hp, ycol], A3[hp, c, 1, :], KBV[hp, c, 2, :],
                                                                    start=True, stop=False), rd=[A3b, KBVb], wr=[pb[yb]])
                            k.op(PE, lambda hp=hp: nc.tensor.matmul(ps[yb][hp, ycol], rtT[hp, tk], Sc[hp, :],
                                                                    start=False, stop=False), rd=[rtb, Scb], wr=[pb[yb]])
                            k.op(PE, lambda hp=hp: nc.tensor.matmul(ps[1][hp, 64:128], KBV[hp, c, 0, :], KBV[hp, c, 2, :],
                                                                    start=True, stop=False), rd=[KBVb], wr=[pb[1]])
                        next(nxt, None)
                        for h in range(2):
                            hp = slice(64 * h, 64 * h + 64)
                            k.op(PE, lambda hp=hp: nc.tensor.matmul(ps[6][hp, 0:64], TT[hp, c, :], Xsb[hp, :],
                                                                    start=True, stop=True), rd=[TTb, Xb], wr=[pb[6]])
                        k.op(DVE, lambda: nc.vector.tensor_copy(Usb[:], ps[6][:, 0:64]), rd=[pb[6]], wr=[Ub])
                        next(nxt, None)
                        for h in range(2):
                            hp = slice(64 * h, 64 * h + 64)
                            k.op(PE, lambda hp=hp: nc.tensor.matmul(ps[1][hp, 64:128], KBV[hp, c, 1, :], Usb[hp, :],
                                                                    start=False, stop=True), rd=[KBVb, Ub], wr=[pb[1]])
                        for h in range(2):
                            hp = slice(64 * h, 64 * h + 64)
                            k.op(PE, lambda hp=hp: nc.tensor.matmul(ps[yb][hp, ycol], A3[hp, c, 2, :], Usb[hp, :],
                                                                    start=False, stop=True), rd=[A3b, Ub], wr=[pb[yb]])
                            k.op(PE, lambda hp=hp: nc.tensor.matmul(ps[yb][hp, 256 + ci:257 + ci], rkT[hp, tk], cones[hp, 0:1],
                                                                    start=True, stop=True), rd=[rkb, cb], wr=[pb[yb]])
                        k.op(DVE, lambda: nc.vector.scalar_tensor_tensor(Sn[:], S32[:], gam[:, c:c + 1], ps[1][:, 64:128],
                                                                         ALU.mult, ALU.add), rd=[S32b, gamb, pb[1]], wr=[Snb])
                        k.op(DVE, lambda: nc.vector.scalar_tensor_tensor(S32[:], S32[:], gam[:, c:c + 1], ps[1][:, 64:128],
                                                                         ALU.mult, ALU.add), rd=[S32b, gamb, pb[1]], wr=[S32b])
                        next(nxt, None)
                        if g >= 2 and ci == 0:
                            post_stage(5, g - 2)
                        if g >= 2 and ci == 1:
                            post_stage(6, g - 2)
                        if g >= 1:
                            post_stage(ci + 1, g - 1)
                    for _ in nxt:
                        pass
                    post_stage(5, 2)
                    post_stage(6, 2)
                    for stg in range(1, 7):
                        post_stage(stg, 3)
                    k.dma(SP, cin_v[1][j][:, t0:t0 + TH], obS[:], rd=[obb])
                    tap('obS%d' % (2 * j + hh), obS[:], 128, TH)
                    if KSTOP <= 6 and 2 * j + hh + 1 >= KITER:
                        raise _Stop()

    k.barrier()
    nc.gpsimd.collective_compute("AllGather", ALU.bypass, replica_groups=RG,
                                 ins=[cinB.ap().opt()], outs=[coutB.ap().opt()]).then_inc(ccsB, 1)
    if KSTOP <= 7:
        raise _Stop()

    TG = T // 2
    with ExitStack() as st:
        fg = sb("fg", [128, D], F32, st)
        fgbuf = Buf()
        k.dma(SP, fg[:], fgb_d[:, :], wr=[fgbuf])
        mT = sb("mT", [128, 16, TG], BF16, st)
        mTb = [Buf() for _ in range(16)]
        with ExitStack() as st1:
            hTg = sb("hTg", [128, 16, TG], BF16, st1)
            with ExitStack() as st2:
                make_hT(hTg, xTg, 2, st2)
                nc.gpsimd.wait_ge(ccsA, 1)
                nc.gpsimd.wait_ge(ccsB, 1)
                k.op(POOL, lambda: nc.gpsimd.memset(omka[:, 0:1], 0.0), wr=[Buf()])
                k.barrier()
                tap('hTg', hTg[:, 0, :], 128, TG)
            ofull = sb("ofull", [128, 16, TG], BF16, st1)
            ofb = Buf()
            oA = [sb("oA%d" % i, [128, TG], BF16, st1) for i in range(2)]
            oB = [sb("oB%d" % i, [128, TG], BF16, st1) for i in range(2)]
            oAb = [Buf() for _ in range(2)]
            wp = [sb("wp%d" % i, [128, 2, 1024], BF16, st1) for i in range(2)]
            wpb = [Buf() for _ in range(2)]
            sg = [sb("sg%d" % i, [128, 2, 2, 512], F32, st1) for i in range(2)]
            sgb = [Buf() for _ in range(2)]
            srcs = [coutA.ap().rearrange("(q p) t -> p q t", p=128), coutB.ap().rearrange("(q p) t -> p q t", p=128)]
            for q in range(16):
                s = q % 2
                src = srcs[q // 8]
                k.dma(SP, oA[s][:], src[:, q % 8, 0:TG], wr=[oAb[s]])
                k.dma(SP, oB[s][:], src[:, q % 8, TG:T], wr=[oAb[s]])
                k.op(DVE, lambda q=q, s=s: nc.vector.tensor_scalar(ofull[:, q, :], oA[s][:], sel[:, 0:1], None, ALU.mult),
                     rd=[oAb[s], cb], wr=[ofb])
                k.op(DVE, lambda q=q, s=s: nc.vector.scalar_tensor_tensor(ofull[:, q, :], oB[s][:], sel[:, 1:2], ofull[:, q, :],
                                                                          ALU.mult, ALU.add), rd=[oAb[s], ofb, cb], wr=[ofb])
            tap('ofull0', ofull[:, 0, :], 128, TG)
            tap('ofull9', ofull[:, 9, :], 128, TG)
            for cc in range(16):
                s = cc % 2
                k.dma(POOL, wp[s][:, 0, :], wpf_d[cc, :, :], wr=[wpb[s]])
                k.dma(POOL, wp[s][:, 1, :], wpr_d[cc, :, :], wr=[wpb[s]])
                for gi in range(2):
                    fc = 28 + cc * 2 + gi
                    wprefetch(fc + 2)
                    w = wbuf[fc % 4]
                    for dc in range(16):
                        for tb in range(2):
                            k.op(PE, lambda dc=dc, gi=gi, w=w, tb=tb: nc.tensor.matmul(
                                ps[gi * 2 + tb][:, :], w[:, dc * 128:(dc + 1) * 128], hTg[:, dc, tb * 512:(tb + 1) * 512],
                                start=(dc == 0), stop=(dc == 15)), rd=[wbb[fc % 4]], wr=[pb[gi * 2 + tb]])
                for pi in range(2):
                    for kc in range(8):
                        for tb in range(2):
                            k.op(PE, lambda pi=pi, kc=kc, tb=tb: nc.tensor.matmul(
                                ps[4 + pi * 2 + tb][:, :], wp[s][:, pi, kc * 128:(kc + 1) * 128],
                                ofull[:, pi * 8 + kc, tb * 512:(tb + 1) * 512],
                                start=(kc == 0), stop=(kc == 7)), rd=[wpb[s], ofb], wr=[pb[4 + pi * 2 + tb]])
                for gi in range(2):
                    for tb in range(2):
                        k.op(ACT, lambda gi=gi, tb=tb: nc.scalar.activation(sg[s][:, gi, tb, :], ps[gi * 2 + tb][:, :], AF.Sigmoid),
                             rd=[pb[gi * 2 + tb]], wr=[sgb[s]])
                for gi in range(2):
                    for tb in range(2):
                        k.op(DVE, lambda gi=gi, tb=tb: nc.vector.tensor_tensor(
                            sg[s][:, gi, tb, :], sg[s][:, gi, tb, :], ps[4 + gi * 2 + tb][:, :], ALU.mult),
                            rd=[sgb[s], pb[4 + gi * 2 + tb]], wr=[sgb[s]])
                k.op(POOL, lambda: nc.gpsimd.tensor_tensor(
                    mT[:, cc, :], sg[s][:, 0, :, :].rearrange("p a b -> p (a b)"),
                    sg[s][:, 1, :, :].rearrange("p a b -> p (a b)"), ALU.add), rd=[sgb[s]], wr=[mTb[cc]])
            tap('mT0', mT[:, 0, :], 128, TG)
            k.barrier()
        wo = [sb("wo%d" % i, [128, 16 * 512], BF16, st) for i in range(2)]
        wob = [Buf() for _ in range(2)]
        xb_ = [sb("xb%d" % i, [128, 512], F32, st) for i in range(4)]
        xbb = [Buf() for _ in range(4)]
        ybuf = sb("ybuf", [128, 8, D], F32, st)
        ybb = [Buf() for _ in range(8)]
        junk = sb("junk", [128, D], BF16, st)
        jb = Buf()
        sts = sb("sts", [128, 16], F32, st)
        stb = [Buf() for _ in range(8)]
        outdeps = []
        xi = 0
        for nb in range(4):
            wi = nb % 2
            k.dma(POOL, wo[wi][:], wout_d[nb, :, :], wr=[wob[wi]], max_dma_last_dim=8192)
            for tt in range(8):
                bk = tt
                row0 = tt * 128
                xs_ = xi % 4
                xi += 1
                k.dma(SP, xb_[xs_][:], xtok[row0:row0 + 128, nb * 512:(nb + 1) * 512], wr=[xbb[xs_]])
                for kc in range(16):
                    k.op(PE, lambda kc=kc, tt=tt, bk=bk: nc.tensor.matmul(
                        ps[bk][:, :], mT[:, kc, tt * 128:(tt + 1) * 128], wo[wi][:, kc * 512:(kc + 1) * 512],
                        start=(kc == 0), stop=(kc == 15)), rd=[mTb[kc], wob[wi]], wr=[pb[bk]])
                k.op(DVE, lambda tt=tt, bk=bk, xs_=xs_: nc.vector.tensor_tensor(
                    ybuf[:, tt, nb * 512:(nb + 1) * 512], ps[bk][:, :], xb_[xs_][:], ALU.add),
                    rd=[pb[bk], xbb[xs_]], wr=[ybb[tt]])
        tap('y0', ybuf[:, 0, :])
        for tt in range(8):
            row0 = tt * 128
            k.op(ACT, lambda tt=tt: nc.scalar.activation(junk[:], ybuf[:, tt, :], AF.Square, accum_out=sts[:, tt:tt + 1]),
                 rd=[ybb[tt]], wr=[jb, stb[tt]])
            k.op(ACT, lambda tt=tt: nc.scalar.activation(sts[:, 8 + tt:9 + tt], sts[:, tt:tt + 1], AF.Sqrt,
                                                         bias=RMS_EPS, scale=1.0 / D), rd=[stb[tt]], wr=[stb[tt]])
            k.op(DVE, lambda tt=tt: nc.vector.reciprocal(sts[:, 8 + tt:9 + tt], sts[:, 8 + tt:9 + tt]),
                 rd=[stb[tt]], wr=[stb[tt]])
            k.op(DVE, lambda tt=tt: nc.vector.scalar_tensor_tensor(
                ybuf[:, tt, :], ybuf[:, tt, :], sts[:, 8 + tt:9 + tt], fg[:], ALU.mult, ALU.mult),
                rd=[ybb[tt], stb[tt], fgbuf], wr=[ybb[tt]])
            outdeps.append(k.dma(SP, out_d[row0:row0 + 128, :], ybuf[:, tt, :], rd=[ybb[tt]]))
        for d in outdeps:
            k.wait(SP, d)
    k.barrier(dma_queues=("sp", "pool"))


_CACHE = {}


def _consts():
    c = np.zeros((128, 2304), np.float32)
    p = np.arange(128)
    a = p % 64
    b64 = np.arange(64)
    c[:, 0:128] = np.eye(128)
    tri = (p[:, None] <= p[None, :]).astype(np.float32)
    c[:, 128:256] = tri
    c[:, 256:384] = 1.0
    c[:, 384:512] = (p[:, None] // 64 == p[None, :] // 64)
    c[:, 512:576] = (a[:, None] == b64[None, :])
    lt = (a[:, None] < b64[None, :]).astype(np.float32)
    gt = (b64[None, :] < a[:, None]).astype(np.float32)
    le = (a[:, None] <= b64[None, :]).astype(np.float32)
    c[:, 576:1088] = np.tile(np.concatenate([lt, gt], 1), (1, 4))
    c[:, 1088:1472] = np.tile(np.concatenate([lt, le, le], 1), (1, 2))
    c[:, 1920:2048] = tri
    c[:, 2048:2176] = 1.0
    rm = np.ones((128, TH), np.float32)
    rm[:, ::CH] = 0.0
    return c, rm


def _tile_fm(W, cols):
    return np.ascontiguousarray(W[:, cols].reshape(16, 128, len(cols)).transpose(1, 0, 2))


def kernel(x, norm_gain, w_in, fox_forget_bias, rwkv_shift_mix, rwkv_w0, rwkv_w2, rwkv_a0, rwkv_a2,
           rwkv_k_k, rwkv_k_a, rwkv_r_k, rwkv_ln_w, rwkv_ln_b, w_proj_fox, w_proj_rwkv, w_out,
           final_norm_gain):
    f = lambda a: np.asarray(a, dtype=np.float32)
    x = f(x)
    W = f(w_in)[0]
    g = f(norm_gain)[0]
    mu = f(rwkv_shift_mix)[0]
    w0, a0, kk_, ka_ = f(rwkv_w0)[0], f(rwkv_a0)[0], f(rwkv_k_k)[0], f(rwkv_k_a)[0]
    rk_ = f(rwkv_r_k)[0].reshape(-1)
    lnw, lnb = f(rwkv_ln_w)[0], f(rwkv_ln_b)[0]
    w2, a2 = f(rwkv_w2)[0], f(rwkv_a2)[0]
    wpf, wpr, wo = f(w_proj_fox)[0], f(w_proj_rwkv)[0], f(w_out)[0]
    fg = f(final_norm_gain)
    fbv = f(fox_forget_bias)[0]
    cst, rmask = _consts()
    ar = np.arange

    wpf_t = np.ascontiguousarray(wpf.reshape(8, 128, 16, 128).transpose(2, 1, 0, 3)).reshape(16, 128, 1024)
    wpr_t = np.ascontiguousarray(wpr.reshape(8, 128, 16, 128).transpose(2, 1, 0, 3)).reshape(16, 128, 1024)
    wo_t = np.ascontiguousarray(wo.reshape(16, 128, 4, 512).transpose(2, 1, 0, 3)).reshape(4, 128, 16 * 512)
    fgb = np.ascontiguousarray(np.broadcast_to(fg[None, :], (128, D)))
    gate_chunks = [_tile_fm(W, G0 + ar(c * 128, (c + 1) * 128)).reshape(128, 2048) for c in range(32)]

    in_maps = []
    for c in range(NCORE):
        b, hf = c // 2, c % 2
        chunks = []
        for j in range(4):
            for base in (0, 1024, 3072):
                chunks.append(_tile_fm(W, base + 512 * hf + ar(128 * j, 128 * j + 128)).reshape(128, 2048))
        for j in range(4):
            for base in (0, 1024, 2048, 3072):
                chunks.append(_tile_fm(W, R0 + base + 512 * hf + ar(128 * j, 128 * j + 128)).reshape(128, 2048))
        wfm = np.stack(chunks + gate_chunks, 0)
        wlora = np.stack([_tile_fm(W, R0 + 4096 + li * 96 + ar(96)).reshape(128, 16 * 96) for li in range(2)], 0)
        vcols = np.concatenate([2048 + 512 * hf + ar(512), 4096 + 4 * hf + ar(4)])
        wv = _tile_fm(W, vcols).reshape(128, 16 * 516)
        pp = np.zeros((128, 64), np.float32)
        pp[:, 0:16] = g.reshape(16, 128).T
        for kind in range(4):
            for j in range(4):
                pp[:, 16 + kind * 4 + j] = mu[kind * 1024 + 512 * hf + 128 * j + ar(128)]
        for li in range(2):
            pp[0:96, 32 + li] = mu[4096 + li * 96 + ar(96)]
        for j in range(4):
            sl = 512 * hf + 128 * j + ar(128)
            pp[:, 34 + j] = w0[sl]
            pp[:, 38 + j] = a0[sl]
            pp[:, 42 + j] = kk_[sl]
            pp[:, 48 + j] = ka_[sl]
            pp[:, 52 + j] = rk_[sl]
        lnwb = np.zeros((4, 128, 64), np.float32)
        lnbb = np.zeros((4, 128, 64), np.float32)
        for j in range(4):
            for h in range(2):
                sl = 512 * hf + 128 * j + 64 * h + ar(64)
                lnwb[j, 64 * h:64 * h + 64, :] = lnw[sl][None, :]
                lnbb[j, 64 * h:64 * h + 64, :] = lnb[sl][None, :]
        sel = np.zeros((128, 2), np.float32)
        sel[:, hf] = 1.0
        xb = x[b]
        in_maps.append({
            "xT": np.ascontiguousarray(xb.T),
            "xTg": np.ascontiguousarray(xb[hf * 1024:(hf + 1) * 1024].T),
            "xtok": np.ascontiguousarray(xb[hf * 1024:(hf + 1) * 1024]),
            "wfm": wfm, "wlora": wlora, "wv": wv,
            "w2": np.ascontiguousarray(w2[:, 512 * hf:512 * hf + 512]),
            "a2": np.ascontiguousarray(a2[:, 512 * hf:512 * hf + 512]),
            "wpf": wpf_t, "wpr": wpr_t, "wout": wo_t, "pp": pp, "lnwb": lnwb, "lnbb": lnbb,
            "fgb": fgb, "fbias": np.ascontiguousarray(np.broadcast_to(fbv[4 * hf:4 * hf + 4][None, :], (128, 4))),
            "cst": cst, "rmask": rmask, "sel": sel,
        })
    if "nc" not in _CACHE:
        _CACHE["nc"] = build()
    res = run_bass_kernel_spmd(_CACHE["nc"], in_maps, core_ids=list(range(NCORE)))
    out = np.zeros((4, T, D), np.float32)
    for c in range(NCORE):
        b, hf = c // 2, c % 2
        out[b, hf * 1024:(hf + 1) * 1024] = np.asarray(res.results[c]["out"])
    return out
```
